# Optimizing a Trainium2 kernel written in Bass

```python
import jax, jax.numpy as jnp
from jax import lax
import numpy as np

D_MODEL = 4096
BATCH = 8
SEQ = 2048
DEPTH = 2

MIX_WIDTH = D_MODEL
RWKV_WIDTH = MIX_WIDTH // 2
RWKV_HEAD_DIM = 64
RWKV_HEADS = RWKV_WIDTH // RWKV_HEAD_DIM
DECAY_LORA = 96
ICLR_LORA = 96
GATE_LORA = 256
RWKV_PROJ = 3 * RWKV_WIDTH + DECAY_LORA + ICLR_LORA + GATE_LORA
LNX_EPS = 64e-5
SBA_WIDTH = MIX_WIDTH - RWKV_WIDTH
SBA_HEAD_DIM = 64
SBA_HEADS = SBA_WIDTH // SBA_HEAD_DIM
SBA_PROJ = 3 * SBA_WIDTH
SBA_BLOCK = 128
W_IN0_COLS = RWKV_PROJ + SBA_PROJ
LRU_WIDTH = D_MODEL
LRU_HEADS = 16
LRU_BLOCK = LRU_WIDTH // LRU_HEADS
LRU_CONV = 4
LRU_C = 8.0
D_FF = 11008
FFN_CONV = 3
LN_EPS = 1e-5
DEEPNORM_ALPHA = (2 * DEPTH) ** 0.25
DEEPNORM_BETA = (8 * DEPTH) ** -0.25

kernel_name = "rwkv7_stickbreak_rglru_convffn_deepnorm"


def layer_norm(x, g, b):
    xf = x.astype(jnp.float32)
    mu = jnp.mean(xf, axis=-1, keepdims=True)
    var = jnp.mean(jnp.square(xf - mu), axis=-1, keepdims=True)
    y = (xf - mu) * lax.rsqrt(var + LN_EPS)
    return (y * g.astype(jnp.float32) + b.astype(jnp.float32)).astype(x.dtype)


def causal_depthwise_conv(x, w, b):
    K = w.shape[0]
    T = x.shape[1]
    xp = jnp.pad(x, ((0, 0), (K - 1, 0), (0, 0)))
    y = b + w[K - 1] * x
    for k in range(K - 1):
        y = y + w[k] * xp[:, k:k + T]
    return y


def token_shift(p):
    return jnp.pad(p, ((0, 0), (1, 0), (0, 0)))[:, :-1]


def rwkv7_mixer(p, shift_mu, decay_base, decay_up, iclr_base, iclr_up, gate_up,
                k_k, k_a, r_k, lnx_g, lnx_b):
    B, T, _ = p.shape
    f32 = jnp.float32
    p = p + (token_shift(p) - p) * shift_mu
    W = RWKV_WIDTH
    r, k, v, wd, ad, gd = jnp.split(
        p, [W, 2 * W, 3 * W, 3 * W + DECAY_LORA, 3 * W + DECAY_LORA + ICLR_LORA], axis=-1)
    w = -jax.nn.softplus(-(decay_base + jnp.tanh(wd) @ decay_up)) - 0.5
    decay = jnp.exp(-jnp.exp(w.astype(f32)))
    a = jax.nn.sigmoid(iclr_base + ad @ iclr_up)
    g = jax.nn.sigmoid(gd) @ gate_up
    heads = lambda t: t.astype(f32).reshape(B, T, RWKV_HEADS, RWKV_HEAD_DIM)
    kk = heads(k * k_k)
    kk = kk * lax.rsqrt(jnp.maximum(jnp.sum(kk * kk, axis=-1, keepdims=True), 1e-24))
    k = k * (1.0 + (a - 1.0) * k_a)
    r_h, k_h, v_h, w_h, a_h = heads(r), heads(k), heads(v), heads(decay), heads(a)
    b_h = kk * a_h

    def step(S, inp):
        r_t, w_t, k_t, v_t, kk_t, b_t = inp
        sa = jnp.einsum('bhvk,bhk->bhv', S, -kk_t)
        S = (S * w_t[:, :, None, :] + sa[..., None] * b_t[:, :, None, :]
             + v_t[..., None] * k_t[:, :, None, :])
        return S, jnp.einsum('bhvk,bhk->bhv', S, r_t)

    xs = tuple(jnp.moveaxis(t, 1, 0) for t in (r_h, w_h, k_h, v_h, kk, b_h))
    S0 = jnp.zeros((B, RWKV_HEADS, RWKV_HEAD_DIM, RWKV_HEAD_DIM), f32)
    _, y = lax.scan(step, S0, xs)
    y = jnp.moveaxis(y, 0, 1)
    mu = jnp.mean(y, axis=-1, keepdims=True)
    var = jnp.mean(jnp.square(y - mu), axis=-1, keepdims=True)
    y = ((y - mu) * lax.rsqrt(var + LNX_EPS)).reshape(B, T, W)
    y = y * lnx_g.astype(f32) + lnx_b.astype(f32)
    bonus = jnp.sum(r_h * k_h * r_k.astype(f32), axis=-1, keepdims=True) * v_h
    y = (y + bonus.reshape(B, T, W)) * g.astype(f32)
    return y.astype(p.dtype)


def stick_breaking_attention(p):
    B, T, _ = p.shape
    f32 = jnp.float32
    q, k, v = jnp.split(p.astype(f32), 3, axis=-1)
    to_heads = lambda t: t.reshape(B, T, SBA_HEADS, SBA_HEAD_DIM).transpose(0, 2, 1, 3)
    q = to_heads(q) * (SBA_HEAD_DIM ** -0.5)
    k, v = to_heads(k), to_heads(v)
    outs = []
    for blk in range(T // SBA_BLOCK):
        start = blk * SBA_BLOCK
        end = start + SBA_BLOCK
        qb = q[:, :, start:end]
        kb, vb = k[:, :, :end], v[:, :, :end]
        z = jnp.einsum('bhqd,bhkd->bhqk', qb, kb)
        t_idx = start + jnp.arange(SBA_BLOCK)[:, None]
        s_idx = jnp.arange(end)[None, :]
        mask = s_idx < t_idx
        log_keep = jnp.where(mask, jax.nn.log_sigmoid(-z), 0.0)
        between = lax.cumsum(log_keep, axis=3, reverse=True) - log_keep
        att = jnp.where(mask, jnp.exp(jax.nn.log_sigmoid(z) + between), 0.0)
        outs.append(jnp.einsum('bhqk,bhkd->bhqd', att, vb))
    o = jnp.concatenate(outs, axis=2)
    return o.transpose(0, 2, 1, 3).reshape(B, T, SBA_WIDTH).astype(p.dtype)


def conv_ffn(x, w_up, conv_w, conv_b, w_down):
    u = causal_depthwise_conv(x @ w_up, conv_w, conv_b)
    gate, val = jnp.split(u, 2, axis=-1)
    return (jax.nn.silu(gate) * val) @ w_down


def rglru_mixer(x, w_in, conv_w, conv_b, gate_r_w, gate_r_b, gate_i_w, gate_i_b, lam, w_out):
    B, T, _ = x.shape
    f32 = jnp.float32
    gate_branch, xb = jnp.split(x @ w_in, 2, axis=-1)
    xb = causal_depthwise_conv(xb, conv_w, conv_b)
    xh = xb.reshape(B, T, LRU_HEADS, LRU_BLOCK)
    r = jax.nn.sigmoid(jnp.einsum('bthi,hij->bthj', xh, gate_r_w).reshape(B, T, LRU_WIDTH) + gate_r_b)
    i = jax.nn.sigmoid(jnp.einsum('bthi,hij->bthj', xh, gate_i_w).reshape(B, T, LRU_WIDTH) + gate_i_b)
    log_a = -LRU_C * r.astype(f32) * jax.nn.softplus(-lam.astype(f32))
    a = jnp.exp(log_a)
    mult = jnp.sqrt(-jnp.expm1(2.0 * log_a))
    mult = jnp.where((jnp.arange(T) == 0)[None, :, None], 1.0, mult)
    u = mult * (i * xb).astype(f32)

    def combine(c1, c2):
        a1, b1 = c1
        a2, b2 = c2
        return a1 * a2, a2 * b1 + b2

    _, h = lax.associative_scan(combine, (a, u), axis=1)
    y = h.astype(x.dtype) * jax.nn.gelu(gate_branch, approximate=True)
    return y @ w_out


def even_layer(x, w_in, shift_mu, decay_base, decay_up, iclr_base, iclr_up, gate_up,
               k_k, k_a, r_k, lnx_g, lnx_b, w_out, ln_mix_g, ln_mix_b,
               ffn_up, ffn_conv_w, ffn_conv_b, ffn_down, ln_ffn_g, ln_ffn_b):
    p = x @ w_in
    y_a = rwkv7_mixer(p[..., :RWKV_PROJ], shift_mu, decay_base, decay_up, iclr_base, iclr_up,
                      gate_up, k_k, k_a, r_k, lnx_g, lnx_b)
    y_b = stick_breaking_attention(p[..., RWKV_PROJ:])
    mix = jnp.concatenate([y_a, y_b], axis=-1) @ w_out
    x = layer_norm(DEEPNORM_ALPHA * x + mix, ln_mix_g, ln_mix_b)
    x = layer_norm(DEEPNORM_ALPHA * x + conv_ffn(x, ffn_up, ffn_conv_w, ffn_conv_b, ffn_down),
                   ln_ffn_g, ln_ffn_b)
    return x


def odd_layer(x, w_in, conv_w, conv_b, gate_r_w, gate_r_b, gate_i_w, gate_i_b, lam, w_out,
              ln_mix_g, ln_mix_b, ffn_up, ffn_conv_w, ffn_conv_b, ffn_down, ln_ffn_g, ln_ffn_b):
    mix = rglru_mixer(x, w_in, conv_w, conv_b, gate_r_w, gate_r_b, gate_i_w, gate_i_b, lam, w_out)
    x = layer_norm(DEEPNORM_ALPHA * x + mix, ln_mix_g, ln_mix_b)
    x = layer_norm(DEEPNORM_ALPHA * x + conv_ffn(x, ffn_up, ffn_conv_w, ffn_conv_b, ffn_down),
                   ln_ffn_g, ln_ffn_b)
    return x


def setup_inputs(seed: int = 0) -> dict:
    key = jax.random.key(seed)
    keys = iter(jax.random.split(key, 64))
    nk = lambda: next(keys)
    nrm = lambda shape, scale: jax.random.normal(nk(), shape, jnp.float32) * scale
    gain = lambda n: 1.0 + nrm((n,), 0.02)
    bias = lambda n: nrm((n,), 0.02)
    d = {}
    d["x"] = nrm((BATCH, SEQ, D_MODEL), 1.0)
    d["l0_w_in"] = nrm((D_MODEL, W_IN0_COLS), D_MODEL ** -0.5)
    d["l0_shift_mu"] = jax.random.uniform(nk(), (RWKV_PROJ,), jnp.float32)
    d["l0_decay_base"] = jax.random.uniform(nk(), (RWKV_WIDTH,), jnp.float32, -5.0, 0.5)
    d["l0_decay_up"] = nrm((DECAY_LORA, RWKV_WIDTH), 0.1)
    d["l0_iclr_base"] = nrm((RWKV_WIDTH,), 0.1)
    d["l0_iclr_up"] = nrm((ICLR_LORA, RWKV_WIDTH), 0.5 * ICLR_LORA ** -0.5)
    d["l0_gate_up"] = nrm((GATE_LORA, RWKV_WIDTH), GATE_LORA ** -0.5)
    d["l0_k_k"] = 0.85 + nrm((RWKV_WIDTH,), 0.05)
    d["l0_k_a"] = 1.0 + nrm((RWKV_WIDTH,), 0.05)
    d["l0_r_k"] = nrm((RWKV_HEADS, RWKV_HEAD_DIM), 0.1)
    d["l0_lnx_g"] = gain(RWKV_WIDTH)
    d["l0_lnx_b"] = bias(RWKV_WIDTH)
    d["l0_w_out"] = nrm((MIX_WIDTH, D_MODEL), DEEPNORM_BETA * MIX_WIDTH ** -0.5)
    d["l0_ln_mix_g"] = gain(D_MODEL)
    d["l0_ln_mix_b"] = bias(D_MODEL)
    d["l0_ffn_up"] = nrm((D_MODEL, 2 * D_FF), D_MODEL ** -0.5)
    d["l0_ffn_conv_w"] = nrm((FFN_CONV, 2 * D_FF), FFN_CONV ** -0.5)
    d["l0_ffn_conv_b"] = bias(2 * D_FF)
    d["l0_ffn_down"] = nrm((D_FF, D_MODEL), DEEPNORM_BETA * D_FF ** -0.5)
    d["l0_ln_ffn_g"] = gain(D_MODEL)
    d["l0_ln_ffn_b"] = bias(D_MODEL)
    d["l1_w_in"] = nrm((D_MODEL, 2 * LRU_WIDTH), D_MODEL ** -0.5)
    d["l1_conv_w"] = nrm((LRU_CONV, LRU_WIDTH), LRU_CONV ** -0.5)
    d["l1_conv_b"] = bias(LRU_WIDTH)
    d["l1_gate_r_w"] = nrm((LRU_HEADS, LRU_BLOCK, LRU_BLOCK), LRU_BLOCK ** -0.5)
    d["l1_gate_r_b"] = bias(LRU_WIDTH)
    d["l1_gate_i_w"] = nrm((LRU_HEADS, LRU_BLOCK, LRU_BLOCK), LRU_BLOCK ** -0.5)
    d["l1_gate_i_b"] = bias(LRU_WIDTH)
    a_target = jax.random.uniform(nk(), (LRU_WIDTH,), jnp.float32, 0.9, 0.999)
    s = a_target ** (1.0 / LRU_C)
    d["l1_lambda"] = jnp.log(s) - jnp.log1p(-s)
    d["l1_w_out"] = nrm((LRU_WIDTH, D_MODEL), DEEPNORM_BETA * LRU_WIDTH ** -0.5)
    d["l1_ln_mix_g"] = gain(D_MODEL)
    d["l1_ln_mix_b"] = bias(D_MODEL)
    d["l1_ffn_up"] = nrm((D_MODEL, 2 * D_FF), D_MODEL ** -0.5)
    d["l1_ffn_conv_w"] = nrm((FFN_CONV, 2 * D_FF), FFN_CONV ** -0.5)
    d["l1_ffn_conv_b"] = bias(2 * D_FF)
    d["l1_ffn_down"] = nrm((D_FF, D_MODEL), DEEPNORM_BETA * D_FF ** -0.5)
    d["l1_ln_ffn_g"] = gain(D_MODEL)
    d["l1_ln_ffn_b"] = bias(D_MODEL)
    return d


def reference(x,
              l0_w_in, l0_shift_mu, l0_decay_base, l0_decay_up, l0_iclr_base, l0_iclr_up,
              l0_gate_up, l0_k_k, l0_k_a, l0_r_k, l0_lnx_g, l0_lnx_b, l0_w_out,
              l0_ln_mix_g, l0_ln_mix_b, l0_ffn_up, l0_ffn_conv_w, l0_ffn_conv_b, l0_ffn_down,
              l0_ln_ffn_g, l0_ln_ffn_b,
              l1_w_in, l1_conv_w, l1_conv_b, l1_gate_r_w, l1_gate_r_b, l1_gate_i_w, l1_gate_i_b,
              l1_lambda, l1_w_out, l1_ln_mix_g, l1_ln_mix_b, l1_ffn_up, l1_ffn_conv_w,
              l1_ffn_conv_b, l1_ffn_down, l1_ln_ffn_g, l1_ln_ffn_b):
    layer_params = (
        (l0_w_in, l0_shift_mu, l0_decay_base, l0_decay_up, l0_iclr_base, l0_iclr_up,
         l0_gate_up, l0_k_k, l0_k_a, l0_r_k, l0_lnx_g, l0_lnx_b, l0_w_out,
         l0_ln_mix_g, l0_ln_mix_b, l0_ffn_up, l0_ffn_conv_w, l0_ffn_conv_b, l0_ffn_down,
         l0_ln_ffn_g, l0_ln_ffn_b),
        (l1_w_in, l1_conv_w, l1_conv_b, l1_gate_r_w, l1_gate_r_b, l1_gate_i_w, l1_gate_i_b,
         l1_lambda, l1_w_out, l1_ln_mix_g, l1_ln_mix_b, l1_ffn_up, l1_ffn_conv_w,
         l1_ffn_conv_b, l1_ffn_down, l1_ln_ffn_g, l1_ln_ffn_b),
    )
    for layer in range(DEPTH):
        if layer % 2 == 0:
            x = even_layer(x, *layer_params[layer])
        else:
            x = odd_layer(x, *layer_params[layer])
    return x
```

```python
import contextlib
import os
import numpy as np
import concourse.bass as bass
import concourse.mybir as mybir
from concourse.bass_utils import run_bass_kernel_spmd

F32 = mybir.dt.float32
BF16 = mybir.dt.bfloat16
AF = mybir.ActivationFunctionType
ALU = mybir.AluOpType
AX = mybir.AxisListType

T = 2048
D = 4096
DFF = 11008
RW = 2048
RPROJ = 6592
W0COLS = 12736
ALPHA = 4.0 ** 0.25
LN_EPS = 1e-5
LNX_EPS = 64e-5


class Buf:
    __slots__ = ("name", "w", "r", "excl")

    def __init__(self, name="", excl=False):
        self.name = name
        self.w = []
        self.r = []
        self.excl = excl


class Op:
    __slots__ = ("eng", "fn", "deps", "needed", "tok", "is_dma", "slot", "prev_same_slot")

    def __init__(self, eng, fn, is_dma):
        self.eng = eng
        self.fn = fn
        self.deps = []
        self.needed = False
        self.tok = None
        self.is_dma = is_dma
        self.slot = None
        self.prev_same_slot = None


COMPUTE = ("pe", "act", "dve", "pool")
QUEUES = ("sync", "pool", "act")


class Sched:
    def __init__(self, nc, dma_ring=12):
        self.nc = nc
        self.ops = {e: [] for e in ("pe", "act", "dve", "pool", "sync")}
        self.ring = dma_ring
        self.dcount = {q: 0 for q in QUEUES}
        self.last_dma = {}
        self.last_c = {}

    def _add(self, eng, fn, reads, writes, is_dma):
        op = Op(eng, fn, is_dma)
        deps = []
        for b in reads:
            deps.extend(b.w)
            if b.excl:
                deps.extend(r for r in b.r if r.eng != eng)
        for b in writes:
            deps.extend(b.w)
            deps.extend(b.r)
        seen = set()
        for d in deps:
            if id(d) in seen:
                continue
            seen.add(id(d))
            if (not is_dma) and (not d.is_dma) and d.eng == eng == "pe":
                continue
            op.deps.append(d)
            d.needed = True
        for b in writes:
            b.w = [op]
            b.r = []
        for b in reads:
            if b not in writes:
                b.r.append(op)
        if is_dma:
            op.needed = True
            op.slot = self.dcount[eng] % self.ring
            self.dcount[eng] += 1
            op.prev_same_slot = self.last_dma.get((eng, op.slot))
            self.last_dma[(eng, op.slot)] = op
        elif fn is not None:
            self.last_c[eng] = op
        self.ops[eng].append(op)
        return op

    def op(self, eng, fn, reads=(), writes=()):
        return self._add(eng, fn, list(reads), list(writes), False)

    def dma(self, queue, fn, reads=(), writes=()):
        return self._add(queue, fn, list(reads), list(writes), True)

    def barrier(self):
        deps = list(self.last_c.values()) + list(self.last_dma.values())
        for d in deps:
            d.needed = True
        for e in self.ops:
            op = Op(e, None, False)
            op.deps = list(deps)
            self.ops[e].append(op)

    def emit(self, final_wait_ops=()):
        nc = self.nc
        with contextlib.ExitStack() as st:
            csem = {e: st.enter_context(nc.semaphore(f"c_{e}")) for e in COMPUTE}
            dsem = {q: [st.enter_context(nc.semaphore(f"d_{q}{i}")) for i in range(self.ring)]
                    for q in QUEUES}
            for e, ops in self.ops.items():
                cnt = 0
                dval = [0] * self.ring
                for op in ops:
                    if op.is_dma:
                        dval[op.slot] += 16
                        op.tok = (dsem[e][op.slot], dval[op.slot])
                    elif op.needed:
                        cnt += 1
                        op.tok = (csem[e], cnt)
            fin = [o.tok for o in final_wait_ops]
            block = st.enter_context(nc.Block())

            def run(engname, eng):
                have = {}
                for op in self.ops[engname]:
                    waits = {}
                    dl = list(op.deps)
                    if op.is_dma and op.prev_same_slot is not None:
                        dl.append(op.prev_same_slot)
                    for d in dl:
                        s, v = d.tok
                        k = id(s)
                        if have.get(k, 0) >= v:
                            continue
                        if k not in waits or waits[k][1] < v:
                            waits[k] = (s, v)
                    for k, (s, v) in waits.items():
                        eng.wait_ge(s, v)
                        have[k] = v
                    if op.fn is None:
                        continue
                    ins = op.fn(eng)
                    if op.is_dma:
                        ins.then_inc(op.tok[0], 16)
                    elif op.needed:
                        ins.then_inc(op.tok[0], 1)
                if engname == "sync":
                    for s, v in fin:
                        eng.wait_ge(s, v)

            @block.sync
            def _(e):
                run("sync", e)

            @block.scalar
            def _(e):
                run("act", e)

            @block.vector
            def _(e):
                run("dve", e)

            @block.gpsimd
            def _(e):
                run("pool", e)

            @block.tensor
            def _(e):
                run("pe", e)


def _dsz(dt):
    return 2 if dt == BF16 else 4


class Ctx:
    def __init__(self, nc, debug_out=()):
        self.nc = nc
        self.S = Sched(nc)
        self.cur = 0
        self.n = 0
        self.debug_out = set(debug_out)
        self.PS = nc.alloc_psum_tensor("ps", [128, 4096], F32)
        self.ARENA_W = 52480
        self.arena = nc.alloc_sbuf_tensor("arena", [128, self.ARENA_W], F32)
        self.psb = [Buf(f"bank{i}", excl=True) for i in range(8)]
        self.cast_rr = 0

    def bank(self, b, m=128, n=512):
        return self.PS[0:m, b * 512:b * 512 + n]

    def sb(self, shape, dtype=F32):
        nbytes = int(np.prod(shape[1:])) * _dsz(dtype)
        nbytes = (nbytes + 63) // 64 * 64
        assert self.cur + nbytes <= self.ARENA_W * 4, f"SBUF overflow {self.cur + nbytes}"
        a = self.arena[0:shape[0], self.cur // 4:(self.cur + nbytes) // 4]
        if dtype != F32:
            a = a.bitcast(dtype)
        a = a[:, 0:int(np.prod(shape[1:]))]
        if len(shape) == 3:
            a = a.rearrange("p (c n) -> p c n", c=shape[1])
        elif len(shape) == 4:
            a = a.rearrange("p (a b n) -> p a b n", a=shape[1], b=shape[2])
        self.n += 1
        self.cur += nbytes
        return a

    def reset(self, keep=0):
        self.S.barrier()
        self.cur = keep

    def dram(self, name, shape, dtype=F32):
        if name in self.debug_out:
            return self.nc.dram_tensor(name, list(shape), dtype, kind="ExternalOutput").ap()
        return self.nc.dram_tensor(name, list(shape), dtype).ap()

    def inp(self, name, shape, dtype=F32):
        return self.nc.dram_tensor(name, list(shape), dtype, kind="ExternalInput").ap()


def load_xb(C, src, KC, t0, NT, XB, xbufs, stage=None, stbufs=None, q="sync"):
    S = C.S
    if src.dtype == BF16:
        step = 8
        for c0 in range(0, KC, step):
            n = min(step, KC - c0)
            sv = src[c0 * 128:(c0 + n) * 128, t0:t0 + NT].rearrange("(c p) t -> p c t", p=128)
            S.dma(q, lambda e, sv=sv, c0=c0, n=n: e.dma_start(out=XB[:, c0:c0 + n, 0:NT], in_=sv),
                  writes=xbufs[c0:c0 + n])
        return
    for c in range(KC):
        sl = c % len(stage)
        st, sbuf = stage[sl], stbufs[sl]
        sv = src[c * 128:(c + 1) * 128, t0:t0 + NT]
        S.dma(q, lambda e, sv=sv, st=st: e.dma_start(out=st[:, 0:NT], in_=sv), writes=[sbuf])
        eng = ("dve", "pool")[c % 2]
        S.op(eng, lambda e, st=st, c=c: e.tensor_copy(out=XB[:, c, 0:NT], in_=st[:, 0:NT]),
             reads=[sbuf], writes=[xbufs[c]])


CAST_ROT = ("act", "dve", "act", "dve", "act", "dve", "pool")


class WStream:
    def __init__(self, C, ring=3, pk=16):
        self.C = C
        self.pk = pk
        self.ring = ring
        self.stage = [C.sb([128, pk * 128], F32) for _ in range(ring)]
        self.wb = [C.sb([128, pk * 128], BF16) for _ in range(ring)]
        self.sbuf = [Buf() for _ in range(ring)]
        self.wbuf = [Buf() for _ in range(ring)]
        self.i = 0

    def load(self, w, k0, n, c0, m):
        C, S = self.C, self.C.S
        sl = self.i % self.ring
        self.i += 1
        sb_, wb_ = self.sbuf[sl], self.wbuf[sl]
        assert n * m <= self.pk * 128
        st = self.stage[sl][:, 0:n * m].rearrange("p (c n) -> p c n", n=m)
        wb = self.wb[sl][:, 0:n * m].rearrange("p (c n) -> p c n", n=m)
        sv = w[k0 * 128:(k0 + n) * 128, c0:c0 + m].rearrange("(c p) n -> p c n", p=128)
        S.dma("sync", lambda e: e.dma_start(out=st[:, 0:n, 0:m], in_=sv), writes=[sb_])
        eng = CAST_ROT[C.cast_rr % len(CAST_ROT)]
        C.cast_rr += 1
        if eng == "act":
            S.op("act", lambda e: e.activation(out=wb[:, 0:n, 0:m], in_=st[:, 0:n, 0:m], func=AF.Identity),
                 reads=[sb_], writes=[wb_])
        else:
            S.op(eng, lambda e: e.tensor_copy(out=wb[:, 0:n, 0:m], in_=st[:, 0:n, 0:m]),
                 reads=[sb_], writes=[wb_])
        return wb, wb_


def stream_linear(C, ws, groups, LA=None):
    S = C.S
    LA = ws.ring - 1 if LA is None else min(LA, ws.ring - 1)
    pieces = []
    for gi, g in enumerate(groups):
        nsub = g.get("nsub", 1)
        pk = ws.pk // nsub
        for p0 in range(0, g["kc"], pk):
            pieces.append((gi, p0, min(pk, g["kc"] - p0)))
    loaded = {}
    for idx in range(len(pieces) + LA):
        if idx < len(pieces):
            gi, p0, n = pieces[idx]
            g = groups[gi]
            loaded[idx] = ws.load(g["w"], p0, n, g["c0"], g["m"])
        j = idx - LA
        if j < 0:
            continue
        gi, p0, n = pieces[j]
        g = groups[gi]
        wb, wbuf = loaded.pop(j)
        nsub = g.get("nsub", 1)
        m = g["m"] // nsub
        for ci in range(n):
            c = p0 + ci
            rhs, rbuf = g["rhs"](c)
            for su in range(nsub):
                bl = g["banks"][su] if nsub > 1 else g["banks"]
                for tt in range(g["nt"]):
                    b = bl[tt]
                    S.op("pe", lambda e, b=b, wb=wb, ci=ci, rhs=rhs, tt=tt, c=c, m=m, su=su, kc=g["kc"]:
                         e.matmul(C.bank(b, m), lhsT=wb[:, ci, su * m:(su + 1) * m], rhs=rhs[:, tt * 512:(tt + 1) * 512],
                                  start=(c == 0), stop=(c == kc - 1)),
                         reads=[wbuf, rbuf], writes=[C.psb[b]])
        if p0 + n == g["kc"] and g.get("epi") is not None:
            g["epi"]()


def phase_inproj0(C, xT, w_in, mu_l, pT):
    S = C.S
    C.reset()
    XB = C.sb([128, 32, T], BF16)
    xbufs = [Buf() for _ in range(32)]
    stage = [C.sb([128, T], F32) for _ in range(2)]
    stbufs = [Buf() for _ in range(2)]
    MU = C.sb([128, 52], F32)
    OM = C.sb([128, 52], F32)
    mub, omb = Buf(), Buf()
    S.dma("sync", lambda e: e.dma_start(out=MU[:], in_=mu_l), writes=[mub])
    S.op("dve", lambda e: e.tensor_scalar(out=OM[:], in0=MU[:], scalar1=-1.0, scalar2=1.0,
                                          op0=ALU.mult, op1=ALU.add), reads=[mub], writes=[omb])
    load_xb(C, xT, 32, 0, T, XB, xbufs, stage, stbufs)
    ws = WStream(C)
    ev = [C.sb([128, T], F32) for _ in range(2)]
    evb = [Buf() for _ in range(2)]
    groups = []
    cols = [(g * 128, 128) for g in range(51)] + [(6528, 64)] + [(RPROJ + g * 128, 128) for g in range(48)]
    for gi, (c0, m) in enumerate(cols):
        banks = [(gi % 2) * 4 + i for i in range(4)]

        def epi(gi=gi, c0=c0, m=m, banks=banks):
            b0 = banks[0]
            psv = C.PS[0:m, b0 * 512:b0 * 512 + T]
            t, tb = ev[gi % 2], evb[gi % 2]
            pbs = [C.psb[b] for b in banks]
            if c0 < RPROJ:
                g = c0 // 128
                S.op("act", lambda e: e.activation(out=t[0:m, :], in_=psv, func=AF.Identity,
                                                   scale=OM[0:m, g:g + 1]),
                     reads=pbs + [omb], writes=[tb])
                S.op("dve", lambda e: e.scalar_tensor_tensor(
                    out=t[0:m, 1:T], in0=C.PS[0:m, b0 * 512:b0 * 512 + T - 1], scalar=MU[0:m, g:g + 1],
                    in1=t[0:m, 1:T], op0=ALU.mult, op1=ALU.add), reads=pbs + [mub, tb], writes=[tb])
            else:
                S.op("act", lambda e: e.activation(out=t[0:m, :], in_=psv, func=AF.Identity),
                     reads=pbs, writes=[tb])
            S.dma("sync", lambda e: e.dma_start(out=pT[c0:c0 + m, :], in_=t[0:m, :]), reads=[tb])

        groups.append(dict(w=w_in, c0=c0, m=m, kc=32, nt=4, banks=banks,
                           rhs=lambda c: (XB[:, c, :], xbufs[c]), epi=epi))
    stream_linear(C, ws, groups)


def phase_linear_res(C, src, w, xres, hout):
    S = C.S
    C.reset()
    XB = C.sb([128, 32, T], BF16)
    xbufs = [Buf() for _ in range(32)]
    stage = stbufs = None
    if src.dtype != BF16:
        stage = [C.sb([128, T], F32) for _ in range(2)]
        stbufs = [Buf() for _ in range(2)]
    load_xb(C, src, 32, 0, T, XB, xbufs, stage, stbufs)
    ws = WStream(C)
    xr = [C.sb([128, T], F32) for _ in range(2)]
    xrb = [Buf() for _ in range(2)]

    def ld(g):
        S.dma("sync", lambda e: e.dma_start(out=xr[g % 2][:, :], in_=xres[g * 128:(g + 1) * 128, :]),
              writes=[xrb[g % 2]])
    ld(0)
    ld(1)
    groups = []
    for g in range(32):
        banks = [(g % 2) * 4 + i for i in range(4)]

        def epi(g=g, banks=banks):
            psv = C.PS[:, banks[0] * 512:banks[0] * 512 + T]
            t, tb = xr[g % 2], xrb[g % 2]
            S.op("dve", lambda e: e.scalar_tensor_tensor(out=t[:, :], in0=t[:, :], scalar=ALPHA, in1=psv,
                                                         op0=ALU.mult, op1=ALU.add),
                 reads=[C.psb[b] for b in banks] + [tb], writes=[tb])
            S.dma("sync", lambda e: e.dma_start(out=hout[g * 128:(g + 1) * 128, :], in_=t[:, :]), reads=[tb])
            if g + 2 < 32:
                ld(g + 2)
        groups.append(dict(w=w, c0=g * 128, m=128, kc=32, nt=4, banks=banks,
                           rhs=lambda c: (XB[:, c, :], xbufs[c]), epi=epi))
    stream_linear(C, ws, groups)


def phase_ln(C, hin, gam_l, bet_l, out_f32=None, out_bf=None):
    S = C.S
    C.reset()
    ONES = C.sb([128, 128], F32)
    onb = Buf()
    S.op("pool", lambda e: e.memset(ONES[:, :], 1.0), writes=[onb])
    GB = C.sb([128, 64], F32)
    gbb = Buf()
    S.dma("sync", lambda e: e.dma_start(out=GB[:, 0:32], in_=gam_l), writes=[gbb])
    S.dma("sync", lambda e: e.dma_start(out=GB[:, 32:64], in_=bet_l), writes=[gbb])
    R = 3
    hb = [C.sb([128, T], F32) for _ in range(R)]
    hbb = [Buf() for _ in range(R)]
    sq = [C.sb([128, T], F32) for _ in range(2)]
    sqb = [Buf() for _ in range(2)]
    for g in range(32):
        h, b = hb[g % R], hbb[g % R]
        S.dma("sync", lambda e, h=h, g=g: e.dma_start(out=h[:, :], in_=hin[g * 128:(g + 1) * 128, :]), writes=[b])
        q, qb = sq[g % 2], sqb[g % 2]
        S.op("act", lambda e, h=h, q=q: e.activation(out=q[:, :], in_=h[:, :], func=AF.Square), reads=[b], writes=[qb])
        for tt in range(4):
            S.op("pe", lambda e, h=h, tt=tt, g=g: e.matmul(C.bank(tt), lhsT=ONES[:, :], rhs=h[:, tt * 512:(tt + 1) * 512],
                                                           start=(g == 0), stop=(g == 31)),
                 reads=[onb, b], writes=[C.psb[tt]])
        for tt in range(4):
            S.op("pe", lambda e, q=q, tt=tt, g=g: e.matmul(C.bank(4 + tt), lhsT=ONES[:, :], rhs=q[:, tt * 512:(tt + 1) * 512],
                                                           start=(g == 0), stop=(g == 31)),
                 reads=[onb, qb], writes=[C.psb[4 + tt]])
    MEAN = C.sb([128, T], F32)
    RSTD = C.sb([128, T], F32)
    NMR = C.sb([128, T], F32)
    mb, rb, nb = Buf(), Buf(), Buf()
    EPS = C.sb([128, 1], F32)
    eb = Buf()
    S.op("pool", lambda e: e.memset(EPS[:, :], LN_EPS), writes=[eb])
    S.op("act", lambda e: e.activation(out=MEAN[:, :], in_=C.PS[:, 0:T], func=AF.Identity, scale=1.0 / D),
         reads=C.psb[0:4], writes=[mb])
    S.op("dve", lambda e: e.tensor_tensor(out=NMR[:, :], in0=MEAN[:, :], in1=MEAN[:, :], op=ALU.mult),
         reads=[mb], writes=[nb])
    S.op("dve", lambda e: e.scalar_tensor_tensor(out=RSTD[:, :], in0=C.PS[:, T:2 * T], scalar=1.0 / D, in1=NMR[:, :],
                                                 op0=ALU.mult, op1=ALU.subtract),
         reads=C.psb[4:8] + [nb], writes=[rb])
    S.op("act", lambda e: e.activation(out=RSTD[:, :], in_=RSTD[:, :], func=AF.Sqrt, bias=EPS[:, 0:1]),
         reads=[rb, eb], writes=[rb])
    S.op("dve", lambda e: e.reciprocal(out=RSTD[:, :], in_=RSTD[:, :]), reads=[rb], writes=[rb])
    S.op("dve", lambda e: e.scalar_tensor_tensor(out=NMR[:, :], in0=MEAN[:, :], scalar=-1.0, in1=RSTD[:, :],
                                                 op0=ALU.mult, op1=ALU.mult),
         reads=[mb, rb], writes=[nb])
    yb_t = [C.sb([128, T], BF16) for _ in range(2)]
    ybb = [Buf() for _ in range(2)]
    for g in range(32):
        h, b = hb[g % R], hbb[g % R]
        S.dma("sync", lambda e, h=h, g=g: e.dma_start(out=h[:, :], in_=hin[g * 128:(g + 1) * 128, :]), writes=[b])
        eng = "dve" if g % 3 else "pool"
        S.op(eng, lambda e, h=h: e.tensor_tensor(out=h[:, :], in0=h[:, :], in1=RSTD[:, :], op=ALU.mult),
             reads=[b, rb], writes=[b])
        S.op(eng, lambda e, h=h: e.tensor_tensor(out=h[:, :], in0=h[:, :], in1=NMR[:, :], op=ALU.add),
             reads=[b, nb], writes=[b])
        if out_bf is not None:
            y, yb_ = yb_t[g % 2], ybb[g % 2]
            S.op("act", lambda e, h=h, y=y, g=g: e.activation(out=y[:, :], in_=h[:, :], func=AF.Identity,
                                                              scale=GB[:, g:g + 1], bias=GB[:, 32 + g:33 + g]),
                 reads=[b, gbb], writes=[yb_])
        S.op("act", lambda e, h=h, g=g: e.activation(out=h[:, :], in_=h[:, :], func=AF.Identity,
                                                     scale=GB[:, g:g + 1], bias=GB[:, 32 + g:33 + g]),
             reads=[b, gbb], writes=[b])
        if out_bf is not None:
            S.dma("sync", lambda e, y=y, g=g: e.dma_start(out=out_bf[g * 128:(g + 1) * 128, :], in_=y[:, :]), reads=[yb_])
        if out_f32 is not None:
            S.dma("sync", lambda e, h=h, g=g: e.dma_start(out=out_f32[g * 128:(g + 1) * 128, :], in_=h[:, :]), reads=[b])


def phase_ffn(C, xbf, xf32, w_up, cw_l, cb_l, w_down, hout):
    S = C.S
    C.reset()
    NT = 1024
    PARTS = ((0, 29), (29, 29), (58, 28))
    XB = C.sb([128, 32, NT], BF16)
    xbufs = [Buf() for _ in range(32)]
    H = C.sb([128, 29, NT], BF16)
    hbufs = [Buf() for _ in range(29)]
    ws = WStream(C, ring=3)
    CW = C.sb([128, 3, 172], F32)
    CB = C.sb([128, 172], F32)
    HALO = C.sb([128, 172, 2], F32)
    cwb, halob = Buf(), [Buf() for _ in range(172)]
    S.dma("sync", lambda e: e.dma_start(out=CW[:, :, :], in_=cw_l), writes=[cwb])
    S.dma("sync", lambda e: e.dma_start(out=CB[:, :], in_=cb_l), writes=[cwb])
    S.op("pool", lambda e: e.memset(HALO[:, :, :], 0.0), writes=halob)
    ub = [C.sb([128, NT + 2], F32) for _ in range(2)]
    ubb = [Buf() for _ in range(2)]
    cb_ = [C.sb([128, NT], F32) for _ in range(2)]
    cbb = [Buf() for _ in range(2)]
    sg = C.sb([128, NT], F32)
    sgb = Buf()
    xr = [C.sb([128, NT], F32) for _ in range(2)]
    xrb = [Buf() for _ in range(2)]
    hob = [Buf() for _ in range(32)]
    groups = []
    load_xb(C, xbf, 32, 0, NT, XB, xbufs)

    def up_epi(j, jl, bg, bv):
        for k, (bank, gidx) in enumerate(((bg, j), (bv, 86 + j))):
            u, ubf = ub[k], ubb[k]
            c, cbf = cb_[k], cbb[k]
            hb_ = halob[gidx]
            pbs = [C.psb[bank], C.psb[bank + 1]]
            S.op("pool", lambda e, u=u, gidx=gidx: e.tensor_copy(out=u[:, 0:2], in_=HALO[:, gidx, :]),
                 reads=[hb_], writes=[ubf])
            S.op("act", lambda e, u=u, bank=bank: e.activation(out=u[:, 2:NT + 2], in_=C.PS[:, bank * 512:bank * 512 + NT],
                                                               func=AF.Identity), reads=pbs, writes=[ubf])
            S.op("pool", lambda e, u=u, gidx=gidx: e.tensor_copy(out=HALO[:, gidx, :], in_=u[:, NT:NT + 2]),
                 reads=[ubf], writes=[hb_])
            S.op("dve", lambda e, u=u, c=c, gidx=gidx: e.tensor_scalar(
                out=c[:, :], in0=u[:, 2:NT + 2], scalar1=CW[:, 2, gidx:gidx + 1], scalar2=CB[:, gidx:gidx + 1],
                op0=ALU.mult, op1=ALU.add), reads=[ubf, cwb], writes=[cbf])
            for kk in (1, 0):
                S.op("dve", lambda e, u=u, c=c, gidx=gidx, kk=kk: e.scalar_tensor_tensor(
                    out=c[:, :], in0=u[:, kk:kk + NT], scalar=CW[:, kk, gidx:gidx + 1], in1=c[:, :],
                    op0=ALU.mult, op1=ALU.add), reads=[ubf, cwb, cbf], writes=[cbf])
        S.op("act", lambda e: e.activation(out=sg[:, :], in_=cb_[0][:, :], func=AF.Silu), reads=[cbb[0]], writes=[sgb])
        S.op("dve", lambda e: e.tensor_tensor(out=H[:, jl, :], in0=sg[:, :], in1=cb_[1][:, :], op=ALU.mult),
             reads=[sgb, cbb[1]], writes=[hbufs[jl]])

    def down_epi(o, bank, t0, part, reload_next):
        t, tb = xr[o % 2], xrb[o % 2]
        pbs = [C.psb[bank], C.psb[bank + 1]]
        psv = C.PS[:, bank * 512:bank * 512 + NT]
        rows = slice(o * 128, (o + 1) * 128)
        if part == 0:
            S.dma("sync", lambda e: e.dma_start(out=t[:, :], in_=xf32[rows, t0:t0 + NT]), writes=[tb])
            S.op("dve", lambda e: e.scalar_tensor_tensor(out=t[:, :], in0=t[:, :], scalar=ALPHA, in1=psv,
                                                         op0=ALU.mult, op1=ALU.add), reads=pbs + [tb], writes=[tb])
        else:
            S.dma("sync", lambda e: e.dma_start(out=t[:, :], in_=hout[rows, t0:t0 + NT]), reads=[hob[o]], writes=[tb])
            S.op("dve", lambda e: e.tensor_tensor(out=t[:, :], in0=t[:, :], in1=psv, op=ALU.add), reads=pbs + [tb], writes=[tb])
        S.dma("sync", lambda e: e.dma_start(out=hout[rows, t0:t0 + NT], in_=t[:, :]), reads=[tb], writes=[hob[o]])
        if reload_next:
            load_xb(C, xbf, 32, t0 + NT, NT, XB, xbufs)

    ntile = T // NT
    for tile in range(ntile):
        t0 = tile * NT
        for part, (g0, ng) in enumerate(PARTS):
            for jl in range(ng):
                j = g0 + jl
                pp = j % 2
                bg, bv = pp * 4, pp * 4 + 2
                groups.append(dict(w=w_up, c0=j * 128, m=128, kc=32, nt=2, banks=[bg, bg + 1],
                                   rhs=lambda c: (XB[:, c, :], xbufs[c]), epi=None))
                groups.append(dict(w=w_up, c0=DFF + j * 128, m=128, kc=32, nt=2, banks=[bv, bv + 1],
                                   rhs=lambda c: (XB[:, c, :], xbufs[c]),
                                   epi=(lambda j=j, jl=jl, bg=bg, bv=bv: up_epi(j, jl, bg, bv))))
            for o in range(32):
                bank = (o % 2) * 2
                reload_next = (o == 0 and part == len(PARTS) - 1 and tile + 1 < ntile)
                groups.append(dict(w=w_down[g0 * 128:(g0 + ng) * 128, :], c0=o * 128, m=128, kc=ng, nt=2, banks=[bank, bank + 1],
                                   rhs=lambda c: (H[:, c, :], hbufs[c]),
                                   epi=(lambda o=o, bank=bank, t0=t0, part=part, rn=reload_next: down_epi(o, bank, t0, part, rn))))
    stream_linear(C, ws, groups)


GELU_NATIVE = False


def phase_inproj1(C, xbf, w_in, cw_l, cb_l, GG, XC, XCb):
    S = C.S
    C.reset()
    XB = C.sb([128, 32, T], BF16)
    xbufs = [Buf() for _ in range(32)]
    load_xb(C, xbf, 32, 0, T, XB, xbufs)
    ws = WStream(C)
    CW = C.sb([128, 4, 32], F32)
    CB = C.sb([128, 32], F32)
    cwb = Buf()
    S.dma("sync", lambda e: e.dma_start(out=CW[:, :, :], in_=cw_l), writes=[cwb])
    S.dma("sync", lambda e: e.dma_start(out=CB[:, :], in_=cb_l), writes=[cwb])
    u = [C.sb([128, T + 3], F32) for _ in range(2)]
    ubf = [Buf() for _ in range(2)]
    for k in range(2):
        S.op("pool", lambda e, k=k: e.memset(u[k][:, 0:3], 0.0), writes=[ubf[k]])
    c_ = [C.sb([128, T], F32) for _ in range(2)]
    cbf = [Buf() for _ in range(2)]
    cb16 = [C.sb([128, T], BF16) for _ in range(2)]
    cb16b = [Buf() for _ in range(2)]
    groups = []
    for gi in range(64):
        banks = [(gi % 2) * 4 + i for i in range(4)]

        def epi(gi=gi, banks=banks):
            par = gi % 2
            psv = C.PS[:, banks[0] * 512:banks[0] * 512 + T]
            pbs = [C.psb[b] for b in banks]
            if gi < 32:
                g = gi
                t, tb = u[par], ubf[par]
                c, cb2 = c_[par], cbf[par]
                if GELU_NATIVE:
                    S.op("act", lambda e: e.activation(out=c[:, :], in_=psv, func=AF.Gelu_apprx_tanh), reads=pbs, writes=[cb2])
                else:
                    S.op("act", lambda e: e.activation(out=t[:, 3:T + 3], in_=psv, func=AF.Square), reads=pbs, writes=[tb])
                    S.op("dve", lambda e: e.tensor_scalar(out=t[:, 3:T + 3], in0=t[:, 3:T + 3], scalar1=0.044715, scalar2=1.0,
                                                          op0=ALU.mult, op1=ALU.add), reads=[tb], writes=[tb])
                    S.op("dve", lambda e: e.tensor_tensor(out=t[:, 3:T + 3], in0=t[:, 3:T + 3], in1=psv, op=ALU.mult),
                         reads=[tb] + pbs, writes=[tb])
                    S.op("act", lambda e: e.activation(out=t[:, 3:T + 3], in_=t[:, 3:T + 3], func=AF.Sigmoid,
                                                       scale=1.5957691216057308), reads=[tb], writes=[tb])
                    S.op("dve", lambda e: e.tensor_tensor(out=c[:, :], in0=t[:, 3:T + 3], in1=psv, op=ALU.mult),
                         reads=[tb] + pbs, writes=[cb2])
                S.dma("sync", lambda e: e.dma_start(out=GG[g * 128:(g + 1) * 128, :], in_=c[:, :]), reads=[cb2])
            else:
                g = gi - 32
                t, tb = u[par], ubf[par]
                c, cb2 = c_[par], cbf[par]
                S.op("act", lambda e: e.activation(out=t[:, 3:T + 3], in_=psv, func=AF.Identity), reads=pbs, writes=[tb])
                S.op("dve", lambda e: e.tensor_scalar(out=c[:, :], in0=t[:, 3:T + 3], scalar1=CW[:, 3, g:g + 1],
                                                      scalar2=CB[:, g:g + 1], op0=ALU.mult, op1=ALU.add),
                     reads=[tb, cwb], writes=[cb2])
                for kk in (2, 1, 0):
                    S.op("dve", lambda e, kk=kk: e.scalar_tensor_tensor(out=c[:, :], in0=t[:, kk:kk + T], scalar=CW[:, kk, g:g + 1],
                                                                        in1=c[:, :], op0=ALU.mult, op1=ALU.add),
                         reads=[tb, cwb, cb2], writes=[cb2])
                S.dma("sync", lambda e: e.dma_start(out=XC[g * 128:(g + 1) * 128, :], in_=c[:, :]), reads=[cb2])
                y, yb_ = cb16[par], cb16b[par]
                S.op("act", lambda e: e.activation(out=y[:, :], in_=c[:, :], func=AF.Identity), reads=[cb2], writes=[yb_])
                S.dma("sync", lambda e: e.dma_start(out=XCb[g * 128:(g + 1) * 128, :], in_=y[:, :]), reads=[yb_])
        groups.append(dict(w=w_in, c0=gi * 128, m=128, kc=32, nt=4, banks=banks,
                           rhs=lambda c: (XB[:, c, :], xbufs[c]), epi=epi))
    stream_linear(C, ws, groups)


def phase_rglru(C, XC, XCb, GG, wr, wi, rb_l, ib_l, lam_l, YL):
    S = C.S
    C.reset()
    PB = C.sb([128, 96], F32)
    pbb = Buf()
    S.dma("sync", lambda e: e.dma_start(out=PB[:, 0:32], in_=rb_l), writes=[pbb])
    S.dma("sync", lambda e: e.dma_start(out=PB[:, 32:64], in_=ib_l), writes=[pbb])
    S.dma("sync", lambda e: e.dma_start(out=PB[:, 64:96], in_=lam_l), writes=[pbb])
    S.op("act", lambda e: e.activation(out=PB[:, 64:96], in_=PB[:, 64:96], func=AF.Exp, scale=-1.0), reads=[pbb], writes=[pbb])
    S.op("act", lambda e: e.activation(out=PB[:, 64:96], in_=PB[:, 64:96], func=AF.Ln, bias=1.0), reads=[pbb], writes=[pbb])
    S.op("dve", lambda e: e.tensor_scalar(out=PB[:, 64:96], in0=PB[:, 64:96], scalar1=-8.0, scalar2=None, op0=ALU.mult),
         reads=[pbb], writes=[pbb])
    wst = [C.sb([128, 2, 2, 256], F32) for _ in range(2)]
    wstb = [Buf() for _ in range(2)]
    wbf = [C.sb([128, 2, 2, 256], BF16) for _ in range(2)]
    wbfb = [Buf() for _ in range(2)]
    xcb = [C.sb([128, 2, T], BF16) for _ in range(2)]
    xcbb = [Buf() for _ in range(2)]
    NB = 2
    xc = [C.sb([128, T], F32) for _ in range(NB)]
    gg = [C.sb([128, T], F32) for _ in range(NB)]
    R_ = [C.sb([128, T], F32) for _ in range(NB)]
    I_ = [C.sb([128, T], F32) for _ in range(NB)]
    A_ = [C.sb([128, T], F32) for _ in range(NB)]
    M_ = [C.sb([128, T], F32) for _ in range(NB)]
    Y_ = [C.sb([128, T], BF16) for _ in range(NB)]
    bufs = [[Buf() for _ in range(7)] for _ in range(NB)]
    for h in range(16):
        hp = h % 2
        S.dma("sync", lambda e, h=h, hp=hp: e.dma_start(out=wst[hp][:, 0, :, :],
                                                       in_=wr[h].rearrange("(c p) j -> p c j", p=128)), writes=[wstb[hp]])
        S.dma("sync", lambda e, h=h, hp=hp: e.dma_start(out=wst[hp][:, 1, :, :],
                                                       in_=wi[h].rearrange("(c p) j -> p c j", p=128)), writes=[wstb[hp]])
        S.op("pool", lambda e, hp=hp: e.tensor_copy(out=wbf[hp][:, :, :, :], in_=wst[hp][:, :, :, :]),
             reads=[wstb[hp]], writes=[wbfb[hp]])
        S.dma("sync", lambda e, h=h, hp=hp: e.dma_start(
            out=xcb[hp][:, :, :], in_=XCb[h * 256:(h + 1) * 256, :].rearrange("(c p) t -> p c t", p=128)), writes=[xcbb[hp]])
        for jg in range(2):
            g = 2 * h + jg
            k = g % NB
            bx, bg, br, bi, ba, bm, by = bufs[k]
            S.dma("sync", lambda e, g=g, k=k: e.dma_start(out=xc[k][:, :], in_=XC[g * 128:(g + 1) * 128, :]), writes=[bx])
            S.dma("sync", lambda e, g=g, k=k: e.dma_start(out=gg[k][:, :], in_=GG[g * 128:(g + 1) * 128, :]), writes=[bg])
            for gi_ in range(2):
                for ic in range(2):
                    for tt in range(4):
                        b = gi_ * 4 + tt
                        S.op("pe", lambda e, hp=hp, gi_=gi_, ic=ic, tt=tt, b=b, jg=jg: e.matmul(
                            C.bank(b), lhsT=wbf[hp][:, gi_, ic, jg * 128:(jg + 1) * 128],
                            rhs=xcb[hp][:, ic, tt * 512:(tt + 1) * 512], start=(ic == 0), stop=(ic == 1)),
                            reads=[wbfb[hp], xcbb[hp]], writes=[C.psb[b]])
            R, I2, A, M, Y = R_[k], I_[k], A_[k], M_[k], Y_[k]
            S.op("act", lambda e, R=R, g=g: e.activation(out=R[:, :], in_=C.PS[:, 0:T], func=AF.Sigmoid, bias=PB[:, g:g + 1]),
                 reads=C.psb[0:4] + [pbb], writes=[br])
            S.op("act", lambda e, I2=I2, g=g: e.activation(out=I2[:, :], in_=C.PS[:, T:2 * T], func=AF.Sigmoid,
                                                           bias=PB[:, 32 + g:33 + g]), reads=C.psb[4:8] + [pbb], writes=[bi])
            S.op("act", lambda e, A=A, R=R, g=g: e.activation(out=A[:, :], in_=R[:, :], func=AF.Exp, scale=PB[:, 64 + g:65 + g]),
                 reads=[br, pbb], writes=[ba])
            S.op("pool", lambda e, M=M, A=A: e.tensor_tensor(out=M[:, :], in0=A[:, :], in1=A[:, :], op=ALU.mult),
                 reads=[ba], writes=[bm])
            S.op("dve", lambda e, M=M: e.tensor_scalar(out=M[:, :], in0=M[:, :], scalar1=-1.0, scalar2=1.0, op0=ALU.mult, op1=ALU.add),
                 reads=[bm], writes=[bm])
            S.op("act", lambda e, M=M: e.activation(out=M[:, :], in_=M[:, :], func=AF.Sqrt), reads=[bm], writes=[bm])
            S.op("pool", lambda e, M=M: e.memset(M[:, 0:1], 1.0), reads=[bm], writes=[bm])
            S.op("pool", lambda e, I2=I2, k=k: e.tensor_tensor(out=I2[:, :], in0=I2[:, :], in1=xc[k][:, :], op=ALU.mult),
                 reads=[bi, bx], writes=[bi])
            S.op("dve", lambda e, I2=I2, M=M: e.tensor_tensor(out=I2[:, :], in0=I2[:, :], in1=M[:, :], op=ALU.mult),
                 reads=[bi, bm], writes=[bi])
            S.op("dve", lambda e, R=R, A=A, I2=I2: e.tensor_tensor_scan(out=R[:, :], data0=A[:, :], data1=I2[:, :], initial=0.0,
                                                                         op0=ALU.mult, op1=ALU.add),
                 reads=[ba, bi, br], writes=[br])
            S.op("dve", lambda e, Y=Y, R=R, k=k: e.tensor_tensor(out=Y[:, :], in0=R[:, :], in1=gg[k][:, :], op=ALU.mult),
                 reads=[br, bg], writes=[by])
            S.dma("sync", lambda e, Y=Y, g=g: e.dma_start(out=YL[g * 128:(g + 1) * 128, :], in_=Y[:, :]), reads=[by])


def make_mask(C, shape, dtype, step, cm, cmp, base=0, rep=1):
    S = C.S
    P, N = shape
    t = C.sb([P, rep * N], dtype)
    b = Buf()
    S.op("pool", lambda e: e.memset(t[:, :], 1.0), writes=[b])
    pat = [[step, N]] if rep == 1 else [[0, rep], [step, N]]
    S.op("pool", lambda e: e.affine_select(out=t[:, :], in_=t[:, :], pattern=pat, compare_op=cmp, fill=0.0, base=base,
                                           channel_multiplier=cm), reads=[b], writes=[b])
    return t, b


def phase_sba(C, pT, YAB):
    S = C.S
    C.reset()
    IDENT, idb = make_mask(C, [128, 128], F32, 1, -1, ALU.is_equal)
    TRI, trb = make_mask(C, [128, 128], F32, -1, 1, ALU.is_ge)
    MSK, mkb = make_mask(C, [128, 128], F32, 1, -1, ALU.is_gt)
    MSKB, mkbb = make_mask(C, [128, 128], BF16, 1, -1, ALU.is_gt)
    ONES = C.sb([128, 128], F32)
    onb = Buf()
    S.op("pool", lambda e: e.memset(ONES[:, :], 1.0), writes=[onb])
    ZW = C.sb([128, 64], BF16)
    ZR = C.sb([128, 512], BF16)
    zb_ = Buf()
    S.op("pool", lambda e: e.memset(ZW[:, :], 0.0), writes=[zb_])
    S.op("pool", lambda e: e.memset(ZR[:, :], 0.0), writes=[zb_])
    stg = [C.sb([128, T], F32) for _ in range(3)]
    stgb = [Buf() for _ in range(3)]
    QB = C.sb([128, T], BF16)
    KB = C.sb([128, T], BF16)
    qbb, kbb = Buf(), Buf()
    VT = C.sb([128, 16, 128], BF16)
    vtb = Buf()
    Rt = [C.sb([128, T], F32) for _ in range(2)]
    Rb = [Buf() for _ in range(2)]
    NP = 16
    Eb = [C.sb([128, 512], F32) for _ in range(NP)]
    Ebb = [Buf() for _ in range(NP)]
    Zs = [C.sb([128, 512], F32) for _ in range(NP)]
    Zsb = [Buf() for _ in range(NP)]
    At = [C.sb([128, 512], BF16) for _ in range(NP)]
    Atb = [Buf() for _ in range(NP)]
    OUT = C.sb([128, T], BF16)
    outb = Buf()
    Dmy = [Buf() for _ in range(NP)]
    pc = 0
    import os
    STG = int(os.environ.get('SBA_STAGE', '9'))
    for hp in range(int(os.environ.get('SBA_HP', '16'))):
        r0 = RPROJ + hp * 128
        for i, off in enumerate((0, 2048, 4096)):
            S.dma("sync", lambda e, i=i, off=off, r0=r0: e.dma_start(out=stg[i][:, :], in_=pT[r0 + off:r0 + off + 128, :]),
                  writes=[stgb[i]])
        S.op("dve", lambda e: e.tensor_copy(out=QB[:, :], in_=stg[0][:, :]), reads=[stgb[0]], writes=[qbb])
        S.op("pool", lambda e: e.tensor_copy(out=KB[:, :], in_=stg[1][:, :]), reads=[stgb[1]], writes=[kbb])
        for jq in range(4):
            for jj in range(4):
                J = jq * 4 + jj
                S.op("pe", lambda e, J=J, jj=jj: e.transpose(out=C.PS[:, jj * 128:(jj + 1) * 128],
                                                            in_=stg[2][:, J * 128:(J + 1) * 128], identity=IDENT[:, :]),
                     reads=[stgb[2], idb], writes=[C.psb[0]])
            S.op("act", lambda e, jq=jq: e.activation(out=VT[:, jq * 4:(jq + 1) * 4, :], in_=C.PS[:, 0:512].rearrange("p (a b) -> p a b", a=4),
                                                      func=AF.Identity), reads=[C.psb[0]], writes=[vtb])
        for hh in range(2):
            S.op("pool", lambda e, hh=hh: e.memset(Rt[hh][:, :], 0.0), writes=[Rb[hh]])
            for b in range(4):
                S.op("pe", lambda e, hh=hh, b=b: e.matmul(C.PS[hh * 64:(hh + 1) * 64, (4 + b) * 512:(5 + b) * 512],
                                                          lhsT=ZW[:, :], rhs=ZR[:, :], start=True, stop=False),
                     reads=[zb_], writes=[C.psb[4 + b]])
        for J in range(15, -1, -1):
            t0 = J * 128
            plist = []
            c0 = t0
            while c0 < T:
                c1 = min((c0 // 512 + 1) * 512, T)
                for hh in range(2):
                    plist.append((c0, c1 - c0, hh, pc % NP, pc % 2))
                    pc += 1
                c0 = c1
            for (c0, n, hh, k, zbank) in plist:
                pb = hh * 64
                E, Eb_, Z, Zb_ = Eb[k], Ebb[k], Zs[k], Zsb[k]
                R, Rb_ = Rt[hh], Rb[hh]
                zps = C.PS[:, zbank * 512:zbank * 512 + n]
                S.op("pe", lambda e, pb=pb, t0=t0, c0=c0, n=n, zps=zps: e.matmul(
                    zps, lhsT=KB[pb:pb + 64, t0:t0 + 128], rhs=QB[pb:pb + 64, c0:c0 + n], start=True, stop=True),
                    reads=[kbb, qbb], writes=[C.psb[zbank]])
                S.op("act", lambda e, E=E, n=n, zps=zps: e.activation(out=E[:, 0:n], in_=zps, func=AF.Exp, scale=0.125),
                     reads=[C.psb[zbank]], writes=[Eb_])
                S.op("dve", lambda e, Z=Z, n=n, zps=zps, R=R, c0=c0: e.scalar_tensor_tensor(
                    out=Z[:, 0:n], in0=zps, scalar=0.125, in1=R[:, c0:c0 + n], op0=ALU.mult, op1=ALU.subtract),
                    reads=[C.psb[zbank], Rb_], writes=[Zb_])
            for (c0, n, hh, k, zbank) in plist:
                E, Eb_, Z, Zb_ = Eb[k], Ebb[k], Zs[k], Zsb[k]
                R, Rb_ = Rt[hh], Rb[hh]
                S.op("act", lambda e, E=E, n=n: e.activation(out=E[:, 0:n], in_=E[:, 0:n], func=AF.Ln, bias=1.0),
                     reads=[Eb_], writes=[Eb_])
                if c0 == t0:
                    S.op("pool", lambda e, E=E: e.tensor_tensor(out=E[:, 0:128], in0=E[:, 0:128], in1=MSK[:, :], op=ALU.mult),
                         reads=[Eb_, mkb], writes=[Eb_])
                S.op("pe", lambda e, E=E, n=n: e.matmul(C.PS[:, 1024:1024 + n], lhsT=TRI[:, :], rhs=E[:, 0:n], start=True, stop=True),
                     reads=[trb, Eb_], writes=[C.psb[2]])
                if J > 0:
                    S.op("pe", lambda e, E=E, n=n: e.matmul(C.PS[:, 1536:1536 + n], lhsT=ONES[:, :], rhs=E[:, 0:n], start=True, stop=True),
                         reads=[onb, Eb_], writes=[C.psb[3]])
                S.op("dve", lambda e, Z=Z, n=n: e.tensor_tensor(out=Z[:, 0:n], in0=Z[:, 0:n], in1=C.PS[:, 1024:1024 + n], op=ALU.subtract),
                     reads=[Zb_, C.psb[2]], writes=[Zb_])
                if J > 0:
                    S.op("dve", lambda e, R=R, c0=c0, n=n: e.tensor_tensor(out=R[:, c0:c0 + n], in0=R[:, c0:c0 + n],
                                                                            in1=C.PS[:, 1536:1536 + n], op=ALU.add),
                         reads=[Rb_, C.psb[3]], writes=[Rb_])
            for (c0, n, hh, k, zbank) in plist:
                pb = hh * 64
                Z, Zb_, A, Ab_ = Zs[k], Zsb[k], At[k], Atb[k]
                S.op("act", lambda e, A=A, Z=Z, n=n: e.activation(out=A[:, 0:n], in_=Z[:, 0:n], func=AF.Exp),
                     reads=[Zb_], writes=[Ab_])
                if c0 == t0:
                    S.op("pool", lambda e, A=A: e.tensor_tensor(out=A[:, 0:128], in0=A[:, 0:128], in1=MSKB[:, :], op=ALU.mult),
                         reads=[Ab_, mkbb], writes=[Ab_])
                ob = 4 + c0 // 512
                S.op("pe", lambda e, pb=pb, hh=hh, J=J, A=A, n=n, c0=c0: e.matmul(
                    C.PS[pb:pb + 64, 2048 + c0:2048 + c0 + n], lhsT=VT[:, J, hh * 64:(hh + 1) * 64], rhs=A[:, 0:n],
                    start=False, stop=(J == 0)), reads=[vtb, Ab_], writes=[C.psb[ob]])
        S.op("act", lambda e: e.activation(out=OUT[:, :], in_=C.PS[:, 2048:4096], func=AF.Identity),
             reads=C.psb[4:8], writes=[outb])
        S.dma("sync", lambda e, hp=hp: e.dma_start(out=YAB[RW + hp * 128:RW + (hp + 1) * 128, :], in_=OUT[:, :]), reads=[outb])


C0 = float(np.exp(-0.5))
CH = 64
NCH = T // CH


def phase_rwkv_prep(C, pT, du, iu, gu, vecs, KKt, Rtl, Kh, Bh, GT, BV, GC):
    S = C.S
    C.reset()
    DU = C.sb([96, RW], F32)
    IU = C.sb([96, RW], F32)
    GU = C.sb([128, 2, RW], F32)
    lb = Buf()
    S.dma("sync", lambda e: e.dma_start(out=DU[:, :], in_=du), writes=[lb])
    S.dma("sync", lambda e: e.dma_start(out=IU[:, :], in_=iu), writes=[lb])
    S.dma("sync", lambda e: e.dma_start(out=GU[:, :, :], in_=gu.rearrange("(c p) n -> p c n", p=128)), writes=[lb])
    VEC = C.sb([128, 7, 16], F32)
    vb = Buf()
    S.dma("sync", lambda e: e.dma_start(out=VEC[:, :, :], in_=vecs), writes=[vb])
    S.op("dve", lambda e: e.tensor_scalar(out=VEC[:, 5, :], in0=VEC[:, 3, :], scalar1=-1.0, scalar2=1.0, op0=ALU.mult, op1=ALU.add),
         reads=[vb], writes=[vb])
    TW = C.sb([96, T], F32)
    AD = C.sb([96, T], F32)
    SG = C.sb([128, 2, T], F32)
    twb, adb, sgb = Buf(), Buf(), Buf()
    S.dma("sync", lambda e: e.dma_start(out=TW[:, :], in_=pT[6144:6240, :]), writes=[twb])
    S.dma("sync", lambda e: e.dma_start(out=AD[:, :], in_=pT[6240:6336, :]), writes=[adb])
    S.dma("sync", lambda e: e.dma_start(out=SG[:, :, :], in_=pT[6336:6592, :].rearrange("(c p) t -> p c t", p=128)), writes=[sgb])
    S.op("act", lambda e: e.activation(out=TW[:, :], in_=TW[:, :], func=AF.Tanh), reads=[twb], writes=[twb])
    S.op("act", lambda e: e.activation(out=SG[:, :, :], in_=SG[:, :, :], func=AF.Sigmoid), reads=[sgb], writes=[sgb])
    BLK = C.sb([128, 128], F32)
    blb = Buf()
    S.op("pool", lambda e: e.memset(BLK[:, :], 0.0), writes=[blb])
    S.op("pool", lambda e: e.memset(BLK[0:64, 0:64], 1.0), reads=[blb], writes=[blb])
    S.op("pool", lambda e: e.memset(BLK[64:128, 64:128], 1.0), reads=[blb], writes=[blb])
    RM = C.sb([128, T], F32)
    rmb = Buf()
    S.op("pool", lambda e: e.memset(RM[:, :], 1.0), writes=[rmb])
    S.op("pool", lambda e: e.memset(RM[:, :].rearrange("p (c n) -> p c n", n=CH)[:, :, 0:1], 0.0), reads=[rmb], writes=[rmb])
    X = [C.sb([128, T], F32) for _ in range(11)]
    xb = [Buf() for _ in range(11)]
    X0b = [C.sb([128, T], F32) for _ in range(3)]
    x0bb = [Buf() for _ in range(3)]
    GCS = [C.sb([128, NCH], F32) for _ in range(2)]
    gcsbb = [Buf() for _ in range(2)]
    PSa, PSb = C.PS[:, 0:T], C.PS[:, T:2 * T]
    pa, pb_ = C.psb[0:4], C.psb[4:8]
    for j in range(16):
        if j % 2 == 0:
            Rr, Kk, Vv, br, bk, bv = X[0], X[1], X[2], xb[0], xb[1], xb[2]
        else:
            Rr, Kk, Vv, br, bk, bv = X0b[0], X0b[1], X0b[2], x0bb[0], x0bb[1], x0bb[2]
        fs = slice(j * 128, (j + 1) * 128)
        S.dma("sync", lambda e, Rr=Rr, fs=fs: e.dma_start(out=Rr[:, :], in_=pT[fs, :]), writes=[br])
        S.dma("sync", lambda e, Kk=Kk, j=j: e.dma_start(out=Kk[:, :], in_=pT[2048 + j * 128:2048 + (j + 1) * 128, :]), writes=[bk])
        S.dma("sync", lambda e, Vv=Vv, j=j: e.dma_start(out=Vv[:, :], in_=pT[4096 + j * 128:4096 + (j + 1) * 128, :]), writes=[bv])
        vec = lambda i, j=j: VEC[:, i, j:j + 1]
        for tt in range(4):
            ts_ = slice(tt * 512, (tt + 1) * 512)
            S.op("pe", lambda e, tt=tt, ts_=ts_, fs=fs: e.matmul(C.bank(tt), lhsT=DU[:, fs], rhs=TW[:, ts_], start=True, stop=True),
                 reads=[lb, twb], writes=[C.psb[tt]])
        for tt in range(4):
            ts_ = slice(tt * 512, (tt + 1) * 512)
            S.op("pe", lambda e, tt=tt, ts_=ts_, fs=fs: e.matmul(C.bank(4 + tt), lhsT=IU[:, fs], rhs=AD[:, ts_], start=True, stop=True),
                 reads=[lb, adb], writes=[C.psb[4 + tt]])
        S.op("act", lambda e, vec=vec: e.activation(out=X[3][:, :], in_=PSa, func=AF.Sigmoid, bias=vec(0)), reads=pa + [vb], writes=[xb[3]])
        S.op("act", lambda e, vec=vec: e.activation(out=X[4][:, :], in_=PSb, func=AF.Sigmoid, bias=vec(1)), reads=pb_ + [vb], writes=[xb[4]])
        S.op("dve", lambda e: e.tensor_tensor_scan(out=X[5][:, :], data0=RM[:, :], data1=X[3][:, :], initial=0.0, op0=ALU.mult, op1=ALU.add),
             reads=[rmb, xb[3]], writes=[xb[5]])
        S.op("act", lambda e: e.activation(out=X[6][:, :], in_=X[5][:, :], func=AF.Exp, scale=-C0), reads=[xb[5]], writes=[xb[6]])
        S.op("act", lambda e: e.activation(out=X[7][:, :], in_=X[5][:, :], func=AF.Exp, scale=C0), reads=[xb[5]], writes=[xb[7]])
        S.op("pool", lambda e: e.tensor_tensor(out=X[3][:, :], in0=X[5][:, :], in1=X[3][:, :], op=ALU.subtract), reads=[xb[5], xb[3]], writes=[xb[3]])
        S.op("act", lambda e: e.activation(out=X[8][:, :], in_=X[3][:, :], func=AF.Exp, scale=-C0), reads=[xb[3]], writes=[xb[8]])
        gcs, gcsb = GCS[j % 2], gcsbb[j % 2]
        S.op("pool", lambda e, gcs=gcs: e.tensor_copy(out=gcs[:, :], in_=X[6][:, :].rearrange("p (c n) -> p c n", n=CH)[:, :, CH - 1]),
             reads=[xb[6]], writes=[gcsb])
        S.dma("sync", lambda e, fs=fs, gcs=gcs: e.dma_start(out=GC[fs, :], in_=gcs[:, :]), reads=[gcsb])
        S.op("dve", lambda e, Kk=Kk, vec=vec: e.tensor_scalar(out=X[9][:, :], in0=Kk[:, :], scalar1=vec(2), scalar2=None, op0=ALU.mult),
             reads=[bk, vb], writes=[xb[9]])
        S.op("act", lambda e: e.activation(out=X[10][:, :], in_=X[9][:, :], func=AF.Square), reads=[xb[9]], writes=[xb[10]])
        for tt in range(4):
            ts_ = slice(tt * 512, (tt + 1) * 512)
            S.op("pe", lambda e, tt=tt, ts_=ts_: e.matmul(C.bank(tt), lhsT=BLK[:, :], rhs=X[10][:, ts_], start=True, stop=True),
                 reads=[blb, xb[10]], writes=[C.psb[tt]])
        S.op("dve", lambda e: e.tensor_scalar(out=X[10][:, :], in0=PSa, scalar1=1e-24, scalar2=None, op0=ALU.max), reads=pa + [xb[10]], writes=[xb[10]])
        S.op("act", lambda e: e.activation(out=X[10][:, :], in_=X[10][:, :], func=AF.Sqrt), reads=[xb[10]], writes=[xb[10]])
        S.op("dve", lambda e: e.reciprocal(out=X[10][:, :], in_=X[10][:, :]), reads=[xb[10]], writes=[xb[10]])
        S.op("dve", lambda e: e.tensor_tensor(out=X[9][:, :], in0=X[9][:, :], in1=X[10][:, :], op=ALU.mult), reads=[xb[9], xb[10]], writes=[xb[9]])
        S.op("dve", lambda e, vec=vec: e.tensor_scalar(out=X[10][:, :], in0=X[4][:, :], scalar1=vec(3), scalar2=vec(5), op0=ALU.mult, op1=ALU.add),
             reads=[xb[4], vb, xb[10]], writes=[xb[10]])
        S.op("pool", lambda e, Kk=Kk: e.tensor_tensor(out=Kk[:, :], in0=Kk[:, :], in1=X[10][:, :], op=ALU.mult), reads=[bk, xb[10]], writes=[bk])
        S.op("dve", lambda e: e.tensor_tensor(out=X[10][:, :], in0=X[9][:, :], in1=X[4][:, :], op=ALU.mult), reads=[xb[9], xb[4], xb[10]], writes=[xb[10]])
        S.op("pool", lambda e: e.tensor_tensor(out=X[9][:, :], in0=X[9][:, :], in1=X[8][:, :], op=ALU.mult), reads=[xb[9], xb[8], xb[10]], writes=[xb[9]])
        S.dma("sync", lambda e, fs=fs: e.dma_start(out=KKt[fs, :], in_=X[9][:, :]), reads=[xb[9]])
        S.op("dve", lambda e: e.tensor_tensor(out=X[10][:, :], in0=X[10][:, :], in1=X[7][:, :], op=ALU.mult), reads=[xb[10], xb[7]], writes=[xb[10]])
        S.dma("sync", lambda e, fs=fs: e.dma_start(out=Bh[fs, :], in_=X[10][:, :]), reads=[xb[10]])
        S.op("pool", lambda e, Rr=Rr, Kk=Kk: e.tensor_tensor(out=X[3][:, :], in0=Rr[:, :], in1=Kk[:, :], op=ALU.mult), reads=[br, bk, xb[3]], writes=[xb[3]])
        S.op("dve", lambda e, vec=vec: e.tensor_scalar(out=X[3][:, :], in0=X[3][:, :], scalar1=vec(4), scalar2=None, op0=ALU.mult),
             reads=[xb[3], vb], writes=[xb[3]])
        for tt in range(4):
            ts_ = slice(tt * 512, (tt + 1) * 512)
            S.op("pe", lambda e, tt=tt, ts_=ts_: e.matmul(C.bank(4 + tt), lhsT=BLK[:, :], rhs=X[3][:, ts_], start=True, stop=True),
                 reads=[blb, xb[3]], writes=[C.psb[4 + tt]])
        S.op("dve", lambda e, Vv=Vv: e.tensor_tensor(out=X[3][:, :], in0=Vv[:, :], in1=PSb, op=ALU.mult), reads=pb_ + [bv, xb[3]], writes=[xb[3]])
        S.dma("sync", lambda e, fs=fs: e.dma_start(out=BV[fs, :], in_=X[3][:, :]), reads=[xb[3]])
        S.op("pool", lambda e, Rr=Rr: e.tensor_tensor(out=Rr[:, :], in0=Rr[:, :], in1=X[6][:, :], op=ALU.mult), reads=[br, xb[6]], writes=[br])
        S.dma("sync", lambda e, fs=fs, Rr=Rr: e.dma_start(out=Rtl[fs, :], in_=Rr[:, :]), reads=[br])
        S.op("dve", lambda e, Kk=Kk: e.tensor_tensor(out=Kk[:, :], in0=Kk[:, :], in1=X[7][:, :], op=ALU.mult), reads=[bk, xb[7]], writes=[bk])
        S.dma("sync", lambda e, fs=fs, Kk=Kk: e.dma_start(out=Kh[fs, :], in_=Kk[:, :]), reads=[bk])
        for tt in range(4):
            ts_ = slice(tt * 512, (tt + 1) * 512)
            for c in range(2):
                S.op("pe", lambda e, tt=tt, ts_=ts_, c=c, fs=fs: e.matmul(C.bank(tt), lhsT=GU[:, c, fs], rhs=SG[:, c, ts_], start=(c == 0), stop=(c == 1)),
                     reads=[lb, sgb], writes=[C.psb[tt]])
        S.op("act", lambda e: e.activation(out=X[5][:, :], in_=PSa, func=AF.Identity), reads=pa + [xb[5]], writes=[xb[5]])
        S.dma("sync", lambda e, fs=fs: e.dma_start(out=GT[fs, :], in_=X[5][:, :]), reads=[xb[5]])


def phase_rwkv_core(C, pT, KKt, Rtl, Kh, Bh, GT, BV, GC, lnx_l, YAB):
    import os
    S = C.S
    C.reset()
    P = 64
    SU, sub = make_mask(C, [P, CH], F32, 1, -1, ALU.is_gt, rep=NCH)
    IUm, iub = make_mask(C, [P, CH], F32, 1, -1, ALU.is_ge, rep=NCH)
    SL, slb = make_mask(C, [P, CH], F32, -1, 1, ALU.is_gt, rep=NCH)
    EYE, eyb = make_mask(C, [P, CH], F32, 1, -1, ALU.is_equal, rep=NCH)
    ID = EYE[:, 0:CH]
    ONE = C.sb([P, P], F32)
    oneb = Buf()
    S.op("pool", lambda e: e.memset(ONE[:, :], 1.0), writes=[oneb])
    EPS = C.sb([P, 1], F32)
    S.op("pool", lambda e: e.memset(EPS[:, :], LNX_EPS), writes=[oneb])
    LN = C.sb([P, 2, 32], F32)
    lnb = Buf()
    S.dma("sync", lambda e: e.dma_start(out=LN[:, :, :], in_=lnx_l), writes=[lnb])
    B = [C.sb([P, T], F32) for _ in range(14)]
    bb = [Buf() for _ in range(14)]
    INB = [C.sb([P, T], F32) for _ in range(5)]
    inbb = [Buf() for _ in range(5)]
    GCt = [C.sb([P, NCH], F32) for _ in range(2)]
    gcb = [Buf() for _ in range(2)]
    A = [C.sb([P, P], F32) for _ in range(2)]
    ab = [Buf() for _ in range(2)]
    RH = [C.sb([P, P], F32) for _ in range(2)]
    rhb = [Buf() for _ in range(2)]
    UN = [C.sb([P, P], F32) for _ in range(2)]
    unb = [Buf() for _ in range(2)]
    OUT = C.sb([P, T], BF16)
    outb = Buf()
    PSa, PSb = C.PS[0:P, 0:T], C.PS[0:P, T:2 * T]
    pa, pb_ = C.psb[0:4], C.psb[4:8]
    cs = lambda c: slice(c * CH, (c + 1) * CH)

    def mm_set(dst_ps, banks, lhs, lb_, rhs, rb_):
        for c in range(NCH):
            S.op("pe", lambda e, c=c: e.matmul(dst_ps[:, cs(c)], lhsT=lhs[:, cs(c)], rhs=rhs[:, cs(c)], start=True, stop=True),
                 reads=[lb_, rb_], writes=[banks[c // 8]])

    nheads = int(os.environ.get("RWKV_HEADS", "32"))
    def do_head(h):
        fs = slice(h * P, (h + 1) * P)
        if (h + int(os.environ.get('RWKV_PAR', '0'))) % 2 == 0:
            tin, tinb = B[0:5], bb[0:5]
        else:
            tin, tinb = INB, inbb
        kkt, rt, kh, bh, vv = tin
        bkkt, brt, bkh, bbh, bvv = tinb
        for t_, b_, src in ((kkt, bkkt, KKt[fs, :]), (rt, brt, Rtl[fs, :]), (kh, bkh, Kh[fs, :]), (bh, bbh, Bh[fs, :]),
                            (vv, bvv, pT[4096 + h * P:4096 + (h + 1) * P, :])):
            S.dma("sync", lambda e, t_=t_, src=src: e.dma_start(out=t_[:, :], in_=src), writes=[b_])
        gct, gcbb = GCt[h % 2], gcb[h % 2]
        S.dma("sync", lambda e, gct=gct, fs=fs: e.dma_start(out=gct[:, :], in_=GC[fs, :]), writes=[gcbb])
        khT, bhT, vT = B[5], B[6], B[7]
        for i_, (src, sb_, dst, db_) in enumerate(((kh, bkh, khT, bb[5]), (bh, bbh, bhT, bb[6]), (vv, bvv, vT, bb[7]))):
            ps_, banks = (PSa, pa) if i_ % 2 == 0 else (PSb, pb_)
            for c in range(NCH):
                S.op("pe", lambda e, c=c, ps_=ps_, src=src: e.transpose(out=ps_[:, cs(c)], in_=src[:, cs(c)], identity=ID),
                     reads=[sb_, eyb], writes=[banks[c // 8]])
            S.op("act", lambda e, dst=dst, ps_=ps_: e.activation(out=dst[:, :], in_=ps_, func=AF.Identity), reads=banks, writes=[db_])
        Xm, XTm, MkT, MrkT, MrbT, W1 = B[8], B[9], B[10], B[11], B[12], B[13]
        specs = ((bh, bbh, kkt, bkkt, SU, sub, Xm, bb[8]), (kkt, bkkt, bh, bbh, SL, slb, XTm, bb[9]),
                 (kh, bkh, kkt, bkkt, SU, sub, MkT, bb[10]), (kh, bkh, rt, brt, IUm, iub, MrkT, bb[11]),
                 (bh, bbh, rt, brt, IUm, iub, MrbT, bb[12]))
        for i_, (l_, lb_, r_, rb_, m_, mb_, d_, db_) in enumerate(specs):
            ps_, banks = (PSb, pb_) if i_ % 2 == 0 else (PSa, pa)
            mm_set(ps_, banks, l_, lb_, r_, rb_)
            S.op("dve", lambda e, d_=d_, ps_=ps_, m_=m_: e.tensor_tensor(out=d_[:, :], in0=ps_, in1=m_[:, :], op=ALU.mult),
                 reads=banks + [mb_], writes=[db_])
        mm_set(PSa, pa, MkT, bb[10], vT, bb[7])
        S.op("act", lambda e: e.activation(out=W1[:, :], in_=PSa, func=AF.Identity), reads=pa, writes=[bb[13]])
        G, gb_ = vv, bvv
        S.op("dve", lambda e, G=G: e.tensor_tensor(out=G[:, :], in0=EYE[:, :], in1=Xm[:, :], op=ALU.subtract),
             reads=[eyb, bb[8], bvv], writes=[gb_])
        Pc, Pcb, PTc, PTcb = Xm, bb[8], XTm, bb[9]
        Pn, Pnb, PTn, PTnb = kh, bkh, bh, bbh
        for r in range(5):
            mm_set(PSb, pb_, Pc, Pcb, PTc, PTcb)
            if r < 4:
                mm_set(PSa, pa, PTc, PTcb, Pc, Pcb)
            S.op("act", lambda e, PTn=PTn: e.activation(out=PTn[:, :], in_=PSb, func=AF.Identity), reads=pb_, writes=[PTnb])
            if r < 4:
                S.op("dve", lambda e, Pn=Pn: e.tensor_copy(out=Pn[:, :], in_=PSa), reads=pa, writes=[Pnb])
            mm_set(PSb, pb_, PTn, PTnb, G, gb_)
            S.op("dve", lambda e, G=G: e.tensor_tensor(out=G[:, :], in0=G[:, :], in1=PSb, op=ALU.add), reads=pb_ + [gb_], writes=[gb_])
            Pc, Pcb, PTc, PTcb, Pn, Pnb, PTn, PTnb = Pn, Pnb, PTn, PTnb, Pc, Pcb, PTc, PTcb
        for c in range(NCH):
            S.op("pe", lambda e, c=c: e.transpose(out=PSa[:, cs(c)], in_=kkt[:, cs(c)], identity=ID),
                 reads=[bkkt, eyb], writes=[pa[c // 8]])
        S.op("act", lambda e: e.activation(out=Xm[:, :], in_=PSa, func=AF.Identity), reads=pa, writes=[bb[8]])
        mm_set(PSb, pb_, Xm, bb[8], G, gb_)
        S.op("dve", lambda e: e.tensor_copy(out=XTm[:, :], in_=PSb), reads=pb_, writes=[bb[9]])
        mm_set(PSa, pa, G, gb_, W1, bb[13])
        S.op("act", lambda e: e.activation(out=W1[:, :], in_=PSa, func=AF.Identity, scale=-1.0), reads=pa, writes=[bb[13]])
        Y, yb_ = MkT, bb[10]
        S.op("pool", lambda e: e.memset(A[0][:, :], 0.0), writes=[ab[0]])
        for c in range(0 if not os.environ.get('RWKV_SKIPREC') else NCH, NCH):
            a0, a0b, a1, a1b = A[c % 2], ab[c % 2], A[(c + 1) % 2], ab[(c + 1) % 2]
            rh, rhb_, un, unb_ = RH[c % 2], rhb[c % 2], UN[c % 2], unb[c % 2]
            k0 = (c % 2) * 4
            ps_rhs = C.PS[0:P, (k0 + 0) * 512:(k0 + 0) * 512 + P]
            ps_u = C.PS[0:P, (k0 + 1) * 512:(k0 + 1) * 512 + P]
            ps_s = C.PS[0:P, (k0 + 2) * 512:(k0 + 2) * 512 + P]
            ps_y = C.PS[0:P, (k0 + 3) * 512:(k0 + 3) * 512 + P]
            b_rhs, b_u, b_s, b_y = C.psb[k0], C.psb[k0 + 1], C.psb[k0 + 2], C.psb[k0 + 3]
            S.op("pe", lambda e, c=c, a0=a0, ps_u=ps_u: e.matmul(ps_u, lhsT=XTm[:, cs(c)], rhs=a0[:, :], start=True, stop=True),
                 reads=[bb[9], a0b], writes=[b_u])
            S.op("pe", lambda e, c=c, ps_s=ps_s: e.matmul(ps_s, lhsT=khT[:, cs(c)], rhs=vT[:, cs(c)], start=True, stop=False),
                 reads=[bb[5], bb[7]], writes=[b_s])
            S.op("pe", lambda e, a0=a0, ps_s=ps_s: e.matmul(ps_s, lhsT=ID, rhs=a0[:, :], start=False, stop=False),
                 reads=[eyb, a0b], writes=[b_s])
            S.op("pe", lambda e, c=c, a0=a0, ps_y=ps_y: e.matmul(ps_y, lhsT=a0[:, :], rhs=rt[:, cs(c)], start=True, stop=False),
                 reads=[a0b, brt], writes=[b_y])
            S.op("pe", lambda e, c=c, ps_y=ps_y: e.matmul(ps_y, lhsT=vT[:, cs(c)], rhs=MrkT[:, cs(c)], start=False, stop=False),
                 reads=[bb[7], bb[11]], writes=[b_y])
            S.op("dve", lambda e, c=c, un=un, ps_u=ps_u: e.tensor_tensor(out=un[:, :], in0=W1[:, cs(c)], in1=ps_u, op=ALU.subtract),
                 reads=[b_u, bb[13]], writes=[unb_])
            S.op("pe", lambda e, c=c, un=un, ps_s=ps_s: e.matmul(ps_s, lhsT=bhT[:, cs(c)], rhs=un[:, :], start=False, stop=True),
                 reads=[bb[6], unb_], writes=[b_s])
            S.op("pe", lambda e, c=c, un=un, ps_y=ps_y: e.matmul(ps_y, lhsT=un[:, :], rhs=MrbT[:, cs(c)], start=False, stop=True),
                 reads=[unb_, bb[12]], writes=[b_y])
            S.op("dve", lambda e, c=c, a1=a1, ps_s=ps_s, gct=gct: e.tensor_scalar(out=a1[:, :], in0=ps_s, scalar1=gct[:, c:c + 1], scalar2=None, op0=ALU.mult),
                 reads=[b_s, gcbb], writes=[a1b])
            S.op("act", lambda e, c=c, ps_y=ps_y: e.activation(out=Y[:, cs(c)], in_=ps_y, func=AF.Identity), reads=[b_y], writes=[yb_])
        if h == 0 and "rdbg" in C.debug_out:
            rdbg = C.dram("rdbg", [P, 10, T])
            for i_, (t_, b_) in enumerate(((G, gb_), (W1, bb[13]), (MrkT, bb[11]), (MrbT, bb[12]), (khT, bb[5]), (bhT, bb[6]),
                                           (vT, bb[7]), (Y, yb_), (kkt, bkkt), (rt, brt))):
                S.dma("sync", lambda e, i_=i_, t_=t_: e.dma_start(out=rdbg[:, i_, :], in_=t_[:, :]), reads=[b_])
        SQ, sqb, MEAN, mnb = kh, bkh, bh, bbh
        S.dma("sync", lambda e, fs=fs: e.dma_start(out=Xm[:, :], in_=BV[fs, :]), writes=[bb[8]])
        S.dma("sync", lambda e, fs=fs: e.dma_start(out=XTm[:, :], in_=GT[fs, :]), writes=[bb[9]])
        S.op("act", lambda e: e.activation(out=SQ[:, :], in_=Y[:, :], func=AF.Square), reads=[yb_], writes=[sqb])
        for tt in range(4):
            ts_ = slice(tt * 512, (tt + 1) * 512)
            S.op("pe", lambda e, tt=tt, ts_=ts_: e.matmul(C.bank(tt, P), lhsT=ONE[:, :], rhs=Y[:, ts_], start=True, stop=True),
                 reads=[oneb, yb_], writes=[C.psb[tt]])
        for tt in range(4):
            ts_ = slice(tt * 512, (tt + 1) * 512)
            S.op("pe", lambda e, tt=tt, ts_=ts_: e.matmul(C.bank(4 + tt, P), lhsT=ONE[:, :], rhs=SQ[:, ts_], start=True, stop=True),
                 reads=[oneb, sqb], writes=[C.psb[4 + tt]])
        S.op("act", lambda e: e.activation(out=MEAN[:, :], in_=PSa, func=AF.Identity, scale=1.0 / P), reads=pa, writes=[mnb])
        S.op("dve", lambda e: e.tensor_tensor(out=SQ[:, :], in0=MEAN[:, :], in1=MEAN[:, :], op=ALU.mult), reads=[mnb, sqb], writes=[sqb])
        S.op("dve", lambda e: e.scalar_tensor_tensor(out=SQ[:, :], in0=PSb, scalar=1.0 / P, in1=SQ[:, :], op0=ALU.mult, op1=ALU.subtract),
             reads=pb_ + [sqb], writes=[sqb])
        S.op("act", lambda e: e.activation(out=SQ[:, :], in_=SQ[:, :], func=AF.Sqrt, bias=EPS[:, 0:1]), reads=[sqb, oneb], writes=[sqb])
        S.op("dve", lambda e: e.reciprocal(out=SQ[:, :], in_=SQ[:, :]), reads=[sqb], writes=[sqb])
        S.op("pool", lambda e: e.tensor_tensor(out=Y[:, :], in0=Y[:, :], in1=MEAN[:, :], op=ALU.subtract), reads=[yb_, mnb], writes=[yb_])
        S.op("dve", lambda e: e.tensor_tensor(out=Y[:, :], in0=Y[:, :], in1=SQ[:, :], op=ALU.mult), reads=[yb_, sqb], writes=[yb_])
        S.op("dve", lambda e, h=h: e.tensor_scalar(out=Y[:, :], in0=Y[:, :], scalar1=LN[:, 0, h:h + 1], scalar2=LN[:, 1, h:h + 1],
                                                   op0=ALU.mult, op1=ALU.add), reads=[yb_, lnb], writes=[yb_])
        S.op("pool", lambda e: e.tensor_tensor(out=Y[:, :], in0=Y[:, :], in1=Xm[:, :], op=ALU.add), reads=[yb_, bb[8]], writes=[yb_])
        S.op("dve", lambda e: e.tensor_tensor(out=OUT[:, :], in0=Y[:, :], in1=XTm[:, :], op=ALU.mult), reads=[yb_, bb[9]], writes=[outb])
        S.dma("sync", lambda e, fs=fs: e.dma_start(out=YAB[fs, :], in_=OUT[:, :]), reads=[outb])
        if os.environ.get('RWKV_BAR'):
            S.barrier()

    for h in range(nheads):
        do_head(h)


def vec_l(v, ngroups):
    o = np.zeros(ngroups * 128, np.float32)
    o[:v.shape[0]] = v
    return np.ascontiguousarray(o.reshape(ngroups, 128).T)


def build(phases=None, ext_in=(), debug_out=()):
    nc = bass.Bass("TRN2", target_bir_lowering=False)
    C = Ctx(nc, debug_out)
    C.ext_in = set(ext_in)
    allp = ("inproj0", "rwkv", "sba", "lin0", "ln0a", "ffn0", "ln0b", "inproj1", "rglru", "lin1", "ln1a", "ffn1", "ln1b")
    phases = allp if phases is None else phases
    I = lambda name, shape, dt=F32: C.inp(name, shape, dt)
    Dm = lambda name, shape, dt=F32: (C.inp(name, shape, dt) if name in C.ext_in else C.dram(name, shape, dt))
    xT = I("xT", [D, T])
    if "inproj0" in phases:
        pT = Dm("pT", [W0COLS, T])
        phase_inproj0(C, xT, I("l0_w_in", [D, W0COLS]), I("l0_mu", [128, 52]), pT)
    if "sba" in phases or "rwkv" in phases:
        if "inproj0" not in phases:
            pT = Dm("pT", [W0COLS, T])
        yab = Dm("yab0", [D, T], BF16)
    if "rwkv" in phases:
        sc = {n: Dm(n, [RW, T]) for n in ("KKt", "Rtl", "Kh", "Bh", "GT", "BV")}
        GC = Dm("GC", [RW, NCH])
        phase_rwkv_prep(C, pT, I("l0_decay_up", [96, RW]), I("l0_iclr_up", [96, RW]), I("l0_gate_up", [256, RW]),
                        I("l0_vecs", [128, 7, 16]), sc["KKt"], sc["Rtl"], sc["Kh"], sc["Bh"], sc["GT"], sc["BV"], GC)
        if "rwkv_prep_only" not in phases:
            phase_rwkv_core(C, pT, sc["KKt"], sc["Rtl"], sc["Kh"], sc["Bh"], sc["GT"], sc["BV"], GC, I("l0_lnx", [64, 2, 32]), yab)
    if "sba" in phases:
        phase_sba(C, pT, yab)
    if "lin0" in phases:
        if not ("sba" in phases or "rwkv" in phases):
            yab = Dm("yab0", [D, T], BF16)
        h0a = Dm("h0a", [D, T])
        phase_linear_res(C, yab, I("l0_w_out", [D, D]), xT, h0a)
    if "ln0a" in phases:
        h0a = Dm("h0a", [D, T]) if "lin0" not in phases else h0a
        x1f, x1b = Dm("x1f", [D, T]), Dm("x1b", [D, T], BF16)
        phase_ln(C, h0a, I("l0_ln_mix_g", [128, 32]), I("l0_ln_mix_b", [128, 32]), x1f, x1b)
    if "ffn0" in phases:
        if "ln0a" not in phases:
            x1f, x1b = Dm("x1f", [D, T]), Dm("x1b", [D, T], BF16)
        h0b = Dm("h0b", [D, T])
        phase_ffn(C, x1b, x1f, I("l0_ffn_up", [D, 2 * DFF]), I("l0_ffn_cw", [128, 3, 172]), I("l0_ffn_cb", [128, 172]),
                  I("l0_ffn_down", [DFF, D]), h0b)
    if "ln0b" in phases:
        if "ffn0" not in phases:
            h0b = Dm("h0b", [D, T])
        x2f, x2b = Dm("x2f", [D, T]), Dm("x2b", [D, T], BF16)
        phase_ln(C, h0b, I("l0_ln_ffn_g", [128, 32]), I("l0_ln_ffn_b", [128, 32]), x2f, x2b)
    if "inproj1" in phases:
        if "ln0b" not in phases:
            x2f, x2b = Dm("x2f", [D, T]), Dm("x2b", [D, T], BF16)
        GG, XC, XCb = Dm("GG", [D, T]), Dm("XC", [D, T]), Dm("XCb", [D, T], BF16)
        phase_inproj1(C, x2b, I("l1_w_in", [D, 2 * D]), I("l1_cw", [128, 4, 32]), I("l1_cb", [128, 32]), GG, XC, XCb)
    if "rglru" in phases:
        YL = Dm("YL", [D, T], BF16)
        phase_rglru(C, XC, XCb, GG, I("l1_gate_r_w", [16, 256, 256]), I("l1_gate_i_w", [16, 256, 256]),
                    I("l1_rb", [128, 32]), I("l1_ib", [128, 32]), I("l1_lam", [128, 32]), YL)
    if "lin1" in phases:
        h1a = Dm("h1a", [D, T])
        phase_linear_res(C, YL, I("l1_w_out", [D, D]), x2f, h1a)
    if "ln1a" in phases:
        x3f, x3b = Dm("x3f", [D, T]), Dm("x3b", [D, T], BF16)
        phase_ln(C, h1a, I("l1_ln_mix_g", [128, 32]), I("l1_ln_mix_b", [128, 32]), x3f, x3b)
    if "ffn1" in phases:
        h1b = Dm("h1b", [D, T])
        phase_ffn(C, x3b, x3f, I("l1_ffn_up", [D, 2 * DFF]), I("l1_ffn_cw", [128, 3, 172]), I("l1_ffn_cb", [128, 172]),
                  I("l1_ffn_down", [DFF, D]), h1b)
    if "ln1b" in phases:
        outT = C.nc.dram_tensor("outT", [D, T], F32, kind="ExternalOutput").ap()
        phase_ln(C, h1b, I("l1_ln_ffn_g", [128, 32]), I("l1_ln_ffn_b", [128, 32]), outT, None)
    C.S.barrier()
    C.S.emit([])
    return nc


def prep_inputs(inputs, b):
    m = {}
    m["xT"] = np.ascontiguousarray(inputs["x"][b].T)
    m["l0_w_in"] = inputs["l0_w_in"]
    m["l0_mu"] = vec_l(inputs["l0_shift_mu"], 52)
    m["l0_w_out"] = inputs["l0_w_out"]
    m["l0_decay_up"] = inputs["l0_decay_up"]
    m["l0_iclr_up"] = inputs["l0_iclr_up"]
    m["l0_gate_up"] = inputs["l0_gate_up"]
    z16 = np.zeros(RW, np.float32)
    m["l0_vecs"] = np.ascontiguousarray(np.stack(
        [vec_l(inputs[n], 16) for n in ("l0_decay_base", "l0_iclr_base", "l0_k_k", "l0_k_a")]
        + [vec_l(inputs["l0_r_k"].reshape(-1), 16), vec_l(z16, 16), vec_l(z16, 16)], axis=1))
    m["l0_lnx"] = np.ascontiguousarray(np.stack([inputs["l0_lnx_g"].reshape(32, 64).T, inputs["l0_lnx_b"].reshape(32, 64).T], axis=1))
    for L in ("l0", "l1"):
        for n in ("ln_mix_g", "ln_mix_b", "ln_ffn_g", "ln_ffn_b"):
            m[f"{L}_{n}"] = vec_l(inputs[f"{L}_{n}"], 32)
        m[f"{L}_ffn_up"] = inputs[f"{L}_ffn_up"]
        m[f"{L}_ffn_down"] = inputs[f"{L}_ffn_down"]
        cw = inputs[f"{L}_ffn_conv_w"]
        m[f"{L}_ffn_cw"] = np.ascontiguousarray(np.stack([vec_l(cw[k], 172) for k in range(3)], axis=1))
        m[f"{L}_ffn_cb"] = vec_l(inputs[f"{L}_ffn_conv_b"], 172)
    m["l1_w_in"] = inputs["l1_w_in"]
    m["l1_w_out"] = inputs["l1_w_out"]
    m["l1_cw"] = np.ascontiguousarray(np.stack([vec_l(inputs["l1_conv_w"][k], 32) for k in range(4)], axis=1))
    m["l1_cb"] = vec_l(inputs["l1_conv_b"], 32)
    m["l1_gate_r_w"] = inputs["l1_gate_r_w"]
    m["l1_gate_i_w"] = inputs["l1_gate_i_w"]
    m["l1_rb"] = vec_l(inputs["l1_gate_r_b"], 32)
    m["l1_ib"] = vec_l(inputs["l1_gate_i_b"], 32)
    m["l1_lam"] = vec_l(inputs["l1_lambda"], 32)
    return m


def kernel(**inputs):
    n = 8
    nc = build()
    shared = prep_inputs(inputs, 0)
    in_maps = []
    for b in range(n):
        m = dict(shared)
        m["xT"] = np.ascontiguousarray(inputs["x"][b].T)
        in_maps.append(m)
    res = run_bass_kernel_spmd(nc, in_maps, core_ids=list(range(n)))
    out = np.stack([np.ascontiguousarray(res.results[b]["outT"].T) for b in range(n)], axis=0)
    return out.astype(np.float32, copy=False)
```

```python
import contextlib
import os
import numpy as np
import concourse.bass as bass
import concourse.mybir as mybir
from concourse.bass_utils import run_bass_kernel_spmd

F32 = mybir.dt.float32
BF16 = mybir.dt.bfloat16
AF = mybir.ActivationFunctionType
ALU = mybir.AluOpType
AX = mybir.AxisListType

T = 2048
D = 4096
DFF = 11008
RW = 2048
RPROJ = 6592
W0COLS = 12736
ALPHA = 4.0 ** 0.25
LN_EPS = 1e-5
LNX_EPS = 64e-5


class Buf:
    __slots__ = ("name", "w", "r", "excl")

    def __init__(self, name="", excl=False):
        self.name = name
        self.w = []
        self.r = []
        self.excl = excl


class Op:
    __slots__ = ("eng", "fn", "deps", "needed", "tok", "is_dma", "slot", "prev_same_slot")

    def __init__(self, eng, fn, is_dma):
        self.eng = eng
        self.fn = fn
        self.deps = []
        self.needed = False
        self.tok = None
        self.is_dma = is_dma
        self.slot = None
        self.prev_same_slot = None


COMPUTE = ("pe", "act", "dve", "pool")
QUEUES = ("sync", "pool", "act")


class Sched:
    def __init__(self, nc, dma_ring=12):
        self.nc = nc
        self.ops = {e: [] for e in ("pe", "act", "dve", "pool", "sync")}
        self.ring = dma_ring
        self.dcount = {q: 0 for q in QUEUES}
        self.last_dma = {}
        self.last_c = {}

    def _add(self, eng, fn, reads, writes, is_dma):
        op = Op(eng, fn, is_dma)
        deps = []
        for b in reads:
            deps.extend(b.w)
            if b.excl:
                deps.extend(r for r in b.r if r.eng != eng)
        for b in writes:
            deps.extend(b.w)
            deps.extend(b.r)
        seen = set()
        for d in deps:
            if id(d) in seen:
                continue
            seen.add(id(d))
            if (not is_dma) and (not d.is_dma) and d.eng == eng == "pe":
                continue
            op.deps.append(d)
            d.needed = True
        for b in writes:
            b.w = [op]
            b.r = []
        for b in reads:
            if b not in writes:
                b.r.append(op)
        if is_dma:
            op.needed = True
            op.slot = self.dcount[eng] % self.ring
            self.dcount[eng] += 1
            op.prev_same_slot = self.last_dma.get((eng, op.slot))
            self.last_dma[(eng, op.slot)] = op
        elif fn is not None:
            self.last_c[eng] = op
        self.ops[eng].append(op)
        return op

    def op(self, eng, fn, reads=(), writes=()):
        return self._add(eng, fn, list(reads), list(writes), False)

    def dma(self, queue, fn, reads=(), writes=()):
        return self._add(queue, fn, list(reads), list(writes), True)

    def barrier(self):
        deps = list(self.last_c.values()) + list(self.last_dma.values())
        for d in deps:
            d.needed = True
        for e in self.ops:
            op = Op(e, None, False)
            op.deps = list(deps)
            self.ops[e].append(op)

    def emit(self, final_wait_ops=()):
        nc = self.nc
        with contextlib.ExitStack() as st:
            csem = {e: st.enter_context(nc.semaphore(f"c_{e}")) for e in COMPUTE}
            dsem = {q: [st.enter_context(nc.semaphore(f"d_{q}{i}")) for i in range(self.ring)]
                    for q in QUEUES}
            for e, ops in self.ops.items():
                cnt = 0
                dval = [0] * self.ring
                for op in ops:
                    if op.is_dma:
                        dval[op.slot] += 16
                        op.tok = (dsem[e][op.slot], dval[op.slot])
                    elif op.needed:
                        cnt += 1
                        op.tok = (csem[e], cnt)
            fin = [o.tok for o in final_wait_ops]
            block = st.enter_context(nc.Block())

            def run(engname, eng):
                have = {}
                for op in self.ops[engname]:
                    waits = {}
                    dl = list(op.deps)
                    if op.is_dma and op.prev_same_slot is not None:
                        dl.append(op.prev_same_slot)
                    for d in dl:
                        s, v = d.tok
                        k = id(s)
                        if have.get(k, 0) >= v:
                            continue
                        if k not in waits or waits[k][1] < v:
                            waits[k] = (s, v)
                    for k, (s, v) in waits.items():
                        eng.wait_ge(s, v)
                        have[k] = v
                    if op.fn is None:
                        continue
                    ins = op.fn(eng)
                    if op.is_dma:
                        ins.then_inc(op.tok[0], 16)
                    elif op.needed:
                        ins.then_inc(op.tok[0], 1)
                if engname == "sync":
                    for s, v in fin:
                        eng.wait_ge(s, v)

            @block.sync
            def _(e):
                run("sync", e)

            @block.scalar
            def _(e):
                run("act", e)

            @block.vector
            def _(e):
                run("dve", e)

            @block.gpsimd
            def _(e):
                run("pool", e)

            @block.tensor
            def _(e):
                run("pe", e)


def _dsz(dt):
    return 2 if dt == BF16 else 4


class Ctx:
    def __init__(self, nc, debug_out=()):
        self.nc = nc
        self.S = Sched(nc)
        self.cur = 0
        self.n = 0
        self.debug_out = set(debug_out)
        self.PS = nc.alloc_psum_tensor("ps", [128, 4096], F32)
        self.ARENA_W = 52480
        self.arena = nc.alloc_sbuf_tensor("arena", [128, self.ARENA_W], F32)
        self.psb = [Buf(f"bank{i}", excl=True) for i in range(8)]
        self.cast_rr = 0

    def bank(self, b, m=128, n=512):
        return self.PS[0:m, b * 512:b * 512 + n]

    def sb(self, shape, dtype=F32):
        nbytes = int(np.prod(shape[1:])) * _dsz(dtype)
        nbytes = (nbytes + 63) // 64 * 64
        assert self.cur + nbytes <= self.ARENA_W * 4, f"SBUF overflow {self.cur + nbytes}"
        a = self.arena[0:shape[0], self.cur // 4:(self.cur + nbytes) // 4]
        if dtype != F32:
            a = a.bitcast(dtype)
        a = a[:, 0:int(np.prod(shape[1:]))]
        if len(shape) == 3:
            a = a.rearrange("p (c n) -> p c n", c=shape[1])
        elif len(shape) == 4:
            a = a.rearrange("p (a b n) -> p a b n", a=shape[1], b=shape[2])
        self.n += 1
        self.cur += nbytes
        return a

    def reset(self, keep=0):
        self.S.barrier()
        self.cur = keep

    def dram(self, name, shape, dtype=F32):
        if name in self.debug_out:
            return self.nc.dram_tensor(name, list(shape), dtype, kind="ExternalOutput").ap()
        return self.nc.dram_tensor(name, list(shape), dtype).ap()

    def inp(self, name, shape, dtype=F32):
        return self.nc.dram_tensor(name, list(shape), dtype, kind="ExternalInput").ap()


def load_xb(C, src, KC, t0, NT, XB, xbufs, stage=None, stbufs=None, q="sync"):
    S = C.S
    if src.dtype == BF16:
        step = 8
        for c0 in range(0, KC, step):
            n = min(step, KC - c0)
            sv = src[c0 * 128:(c0 + n) * 128, t0:t0 + NT].rearrange("(c p) t -> p c t", p=128)
            S.dma(q, lambda e, sv=sv, c0=c0, n=n: e.dma_start(out=XB[:, c0:c0 + n, 0:NT], in_=sv),
                  writes=xbufs[c0:c0 + n])
        return
    for c in range(KC):
        sl = c % len(stage)
        st, sbuf = stage[sl], stbufs[sl]
        sv = src[c * 128:(c + 1) * 128, t0:t0 + NT]
        S.dma(q, lambda e, sv=sv, st=st: e.dma_start(out=st[:, 0:NT], in_=sv), writes=[sbuf])
        eng = ("dve", "pool")[c % 2]
        S.op(eng, lambda e, st=st, c=c: e.tensor_copy(out=XB[:, c, 0:NT], in_=st[:, 0:NT]),
             reads=[sbuf], writes=[xbufs[c]])


CAST_ROT = ("act", "dve", "act", "dve", "act", "dve", "pool")


class WStream:
    def __init__(self, C, ring=3, pk=16):
        self.C = C
        self.pk = pk
        self.ring = ring
        self.stage = [C.sb([128, pk * 128], F32) for _ in range(ring)]
        self.wb = [C.sb([128, pk * 128], BF16) for _ in range(ring)]
        self.sbuf = [Buf() for _ in range(ring)]
        self.wbuf = [Buf() for _ in range(ring)]
        self.i = 0

    def load(self, w, k0, n, c0, m):
        C, S = self.C, self.C.S
        sl = self.i % self.ring
        self.i += 1
        sb_, wb_ = self.sbuf[sl], self.wbuf[sl]
        assert n * m <= self.pk * 128
        st = self.stage[sl][:, 0:n * m].rearrange("p (c n) -> p c n", n=m)
        wb = self.wb[sl][:, 0:n * m].rearrange("p (c n) -> p c n", n=m)
        sv = w[k0 * 128:(k0 + n) * 128, c0:c0 + m].rearrange("(c p) n -> p c n", p=128)
        S.dma("sync", lambda e: e.dma_start(out=st[:, 0:n, 0:m], in_=sv), writes=[sb_])
        eng = CAST_ROT[C.cast_rr % len(CAST_ROT)]
        C.cast_rr += 1
        if eng == "act":
            S.op("act", lambda e: e.activation(out=wb[:, 0:n, 0:m], in_=st[:, 0:n, 0:m], func=AF.Identity),
                 reads=[sb_], writes=[wb_])
        else:
            S.op(eng, lambda e: e.tensor_copy(out=wb[:, 0:n, 0:m], in_=st[:, 0:n, 0:m]),
                 reads=[sb_], writes=[wb_])
        return wb, wb_


def stream_linear(C, ws, groups, LA=None):
    S = C.S
    LA = ws.ring - 1 if LA is None else min(LA, ws.ring - 1)
    pieces = []
    for gi, g in enumerate(groups):
        nsub = g.get("nsub", 1)
        pk = ws.pk // nsub
        for p0 in range(0, g["kc"], pk):
            pieces.append((gi, p0, min(pk, g["kc"] - p0)))
    loaded = {}
    for idx in range(len(pieces) + LA):
        if idx < len(pieces):
            gi, p0, n = pieces[idx]
            g = groups[gi]
            loaded[idx] = ws.load(g["w"], p0, n, g["c0"], g["m"])
        j = idx - LA
        if j < 0:
            continue
        gi, p0, n = pieces[j]
        g = groups[gi]
        wb, wbuf = loaded.pop(j)
        nsub = g.get("nsub", 1)
        m = g["m"] // nsub
        for ci in range(n):
            c = p0 + ci
            rhs, rbuf = g["rhs"](c)
            for su in range(nsub):
                bl = g["banks"][su] if nsub > 1 else g["banks"]
                for tt in range(g["nt"]):
                    b = bl[tt]
                    S.op("pe", lambda e, b=b, wb=wb, ci=ci, rhs=rhs, tt=tt, c=c, m=m, su=su, kc=g["kc"]:
                         e.matmul(C.bank(b, m), lhsT=wb[:, ci, su * m:(su + 1) * m], rhs=rhs[:, tt * 512:(tt + 1) * 512],
                                  start=(c == 0), stop=(c == kc - 1)),
                         reads=[wbuf, rbuf], writes=[C.psb[b]])
        if p0 + n == g["kc"] and g.get("epi") is not None:
            g["epi"]()


def phase_inproj0(C, xT, w_in, mu_l, pT):
    S = C.S
    C.reset()
    XB = C.sb([128, 32, T], BF16)
    xbufs = [Buf() for _ in range(32)]
    stage = [C.sb([128, T], F32) for _ in range(2)]
    stbufs = [Buf() for _ in range(2)]
    MU = C.sb([128, 52], F32)
    OM = C.sb([128, 52], F32)
    mub, omb = Buf(), Buf()
    S.dma("sync", lambda e: e.dma_start(out=MU[:], in_=mu_l), writes=[mub])
    S.op("dve", lambda e: e.tensor_scalar(out=OM[:], in0=MU[:], scalar1=-1.0, scalar2=1.0,
                                          op0=ALU.mult, op1=ALU.add), reads=[mub], writes=[omb])
    load_xb(C, xT, 32, 0, T, XB, xbufs, stage, stbufs)
    ws = WStream(C)
    ev = [C.sb([128, T], F32) for _ in range(2)]
    evb = [Buf() for _ in range(2)]
    groups = []
    cols = [(g * 128, 128) for g in range(51)] + [(6528, 64)] + [(RPROJ + g * 128, 128) for g in range(48)]
    for gi, (c0, m) in enumerate(cols):
        banks = [(gi % 2) * 4 + i for i in range(4)]

        def epi(gi=gi, c0=c0, m=m, banks=banks):
            b0 = banks[0]
            psv = C.PS[0:m, b0 * 512:b0 * 512 + T]
            t, tb = ev[gi % 2], evb[gi % 2]
            pbs = [C.psb[b] for b in banks]
            if c0 < RPROJ:
                g = c0 // 128
                S.op("act", lambda e: e.activation(out=t[0:m, :], in_=psv, func=AF.Identity,
                                                   scale=OM[0:m, g:g + 1]),
                     reads=pbs + [omb], writes=[tb])
                S.op("dve", lambda e: e.scalar_tensor_tensor(
                    out=t[0:m, 1:T], in0=C.PS[0:m, b0 * 512:b0 * 512 + T - 1], scalar=MU[0:m, g:g + 1],
                    in1=t[0:m, 1:T], op0=ALU.mult, op1=ALU.add), reads=pbs + [mub, tb], writes=[tb])
            else:
                S.op("act", lambda e: e.activation(out=t[0:m, :], in_=psv, func=AF.Identity),
                     reads=pbs, writes=[tb])
            S.dma("sync", lambda e: e.dma_start(out=pT[c0:c0 + m, :], in_=t[0:m, :]), reads=[tb])

        groups.append(dict(w=w_in, c0=c0, m=m, kc=32, nt=4, banks=banks,
                           rhs=lambda c: (XB[:, c, :], xbufs[c]), epi=epi))
    stream_linear(C, ws, groups)


def phase_linear_res(C, src, w, xres, hout):
    S = C.S
    C.reset()
    XB = C.sb([128, 32, T], BF16)
    xbufs = [Buf() for _ in range(32)]
    stage = stbufs = None
    if src.dtype != BF16:
        stage = [C.sb([128, T], F32) for _ in range(2)]
        stbufs = [Buf() for _ in range(2)]
    load_xb(C, src, 32, 0, T, XB, xbufs, stage, stbufs)
    ws = WStream(C)
    xr = [C.sb([128, T], F32) for _ in range(2)]
    xrb = [Buf() for _ in range(2)]

    def ld(g):
        S.dma("sync", lambda e: e.dma_start(out=xr[g % 2][:, :], in_=xres[g * 128:(g + 1) * 128, :]),
              writes=[xrb[g % 2]])
    ld(0)
    ld(1)
    groups = []
    for g in range(32):
        banks = [(g % 2) * 4 + i for i in range(4)]

        def epi(g=g, banks=banks):
            psv = C.PS[:, banks[0] * 512:banks[0] * 512 + T]
            t, tb = xr[g % 2], xrb[g % 2]
            S.op("dve", lambda e: e.scalar_tensor_tensor(out=t[:, :], in0=t[:, :], scalar=ALPHA, in1=psv,
                                                         op0=ALU.mult, op1=ALU.add),
                 reads=[C.psb[b] for b in banks] + [tb], writes=[tb])
            S.dma("sync", lambda e: e.dma_start(out=hout[g * 128:(g + 1) * 128, :], in_=t[:, :]), reads=[tb])
            if g + 2 < 32:
                ld(g + 2)
        groups.append(dict(w=w, c0=g * 128, m=128, kc=32, nt=4, banks=banks,
                           rhs=lambda c: (XB[:, c, :], xbufs[c]), epi=epi))
    stream_linear(C, ws, groups)


def phase_ln(C, hin, gam_l, bet_l, out_f32=None, out_bf=None):
    S = C.S
    C.reset()
    ONES = C.sb([128, 128], F32)
    onb = Buf()
    S.op("pool", lambda e: e.memset(ONES[:, :], 1.0), writes=[onb])
    GB = C.sb([128, 64], F32)
    gbb = Buf()
    S.dma("sync", lambda e: e.dma_start(out=GB[:, 0:32], in_=gam_l), writes=[gbb])
    S.dma("sync", lambda e: e.dma_start(out=GB[:, 32:64], in_=bet_l), writes=[gbb])
    R = 3
    hb = [C.sb([128, T], F32) for _ in range(R)]
    hbb = [Buf() for _ in range(R)]
    sq = [C.sb([128, T], F32) for _ in range(2)]
    sqb = [Buf() for _ in range(2)]
    for g in range(32):
        h, b = hb[g % R], hbb[g % R]
        S.dma("sync", lambda e, h=h, g=g: e.dma_start(out=h[:, :], in_=hin[g * 128:(g + 1) * 128, :]), writes=[b])
        q, qb = sq[g % 2], sqb[g % 2]
        S.op("act", lambda e, h=h, q=q: e.activation(out=q[:, :], in_=h[:, :], func=AF.Square), reads=[b], writes=[qb])
        for tt in range(4):
            S.op("pe", lambda e, h=h, tt=tt, g=g: e.matmul(C.bank(tt), lhsT=ONES[:, :], rhs=h[:, tt * 512:(tt + 1) * 512],
                                                           start=(g == 0), stop=(g == 31)),
                 reads=[onb, b], writes=[C.psb[tt]])
        for tt in range(4):
            S.op("pe", lambda e, q=q, tt=tt, g=g: e.matmul(C.bank(4 + tt), lhsT=ONES[:, :], rhs=q[:, tt * 512:(tt + 1) * 512],
                                                           start=(g == 0), stop=(g == 31)),
                 reads=[onb, qb], writes=[C.psb[4 + tt]])
    MEAN = C.sb([128, T], F32)
    RSTD = C.sb([128, T], F32)
    NMR = C.sb([128, T], F32)
    mb, rb, nb = Buf(), Buf(), Buf()
    EPS = C.sb([128, 1], F32)
    eb = Buf()
    S.op("pool", lambda e: e.memset(EPS[:, :], LN_EPS), writes=[eb])
    S.op("act", lambda e: e.activation(out=MEAN[:, :], in_=C.PS[:, 0:T], func=AF.Identity, scale=1.0 / D),
         reads=C.psb[0:4], writes=[mb])
    S.op("dve", lambda e: e.tensor_tensor(out=NMR[:, :], in0=MEAN[:, :], in1=MEAN[:, :], op=ALU.mult),
         reads=[mb], writes=[nb])
    S.op("dve", lambda e: e.scalar_tensor_tensor(out=RSTD[:, :], in0=C.PS[:, T:2 * T], scalar=1.0 / D, in1=NMR[:, :],
                                                 op0=ALU.mult, op1=ALU.subtract),
         reads=C.psb[4:8] + [nb], writes=[rb])
    S.op("act", lambda e: e.activation(out=RSTD[:, :], in_=RSTD[:, :], func=AF.Sqrt, bias=EPS[:, 0:1]),
         reads=[rb, eb], writes=[rb])
    S.op("dve", lambda e: e.reciprocal(out=RSTD[:, :], in_=RSTD[:, :]), reads=[rb], writes=[rb])
    S.op("dve", lambda e: e.scalar_tensor_tensor(out=NMR[:, :], in0=MEAN[:, :], scalar=-1.0, in1=RSTD[:, :],
                                                 op0=ALU.mult, op1=ALU.mult),
         reads=[mb, rb], writes=[nb])
    yb_t = [C.sb([128, T], BF16) for _ in range(2)]
    ybb = [Buf() for _ in range(2)]
    for g in range(32):
        h, b = hb[g % R], hbb[g % R]
        S.dma("sync", lambda e, h=h, g=g: e.dma_start(out=h[:, :], in_=hin[g * 128:(g + 1) * 128, :]), writes=[b])
        eng = "dve" if g % 3 else "pool"
        S.op(eng, lambda e, h=h: e.tensor_tensor(out=h[:, :], in0=h[:, :], in1=RSTD[:, :], op=ALU.mult),
             reads=[b, rb], writes=[b])
        S.op(eng, lambda e, h=h: e.tensor_tensor(out=h[:, :], in0=h[:, :], in1=NMR[:, :], op=ALU.add),
             reads=[b, nb], writes=[b])
        if out_bf is not None:
            y, yb_ = yb_t[g % 2], ybb[g % 2]
            S.op("act", lambda e, h=h, y=y, g=g: e.activation(out=y[:, :], in_=h[:, :], func=AF.Identity,
                                                              scale=GB[:, g:g + 1], bias=GB[:, 32 + g:33 + g]),
                 reads=[b, gbb], writes=[yb_])
        S.op("act", lambda e, h=h, g=g: e.activation(out=h[:, :], in_=h[:, :], func=AF.Identity,
                                                     scale=GB[:, g:g + 1], bias=GB[:, 32 + g:33 + g]),
             reads=[b, gbb], writes=[b])
        if out_bf is not None:
            S.dma("sync", lambda e, y=y, g=g: e.dma_start(out=out_bf[g * 128:(g + 1) * 128, :], in_=y[:, :]), reads=[yb_])
        if out_f32 is not None:
            S.dma("sync", lambda e, h=h, g=g: e.dma_start(out=out_f32[g * 128:(g + 1) * 128, :], in_=h[:, :]), reads=[b])


def phase_ffn(C, xbf, xf32, w_up, cw_l, cb_l, w_down, hout):
    S = C.S
    C.reset()
    NT = 1024
    PARTS = ((0, 29), (29, 29), (58, 28))
    XB = C.sb([128, 32, NT], BF16)
    xbufs = [Buf() for _ in range(32)]
    H = C.sb([128, 29, NT], BF16)
    hbufs = [Buf() for _ in range(29)]
    ws = WStream(C, ring=3)
    CW = C.sb([128, 3, 172], F32)
    CB = C.sb([128, 172], F32)
    HALO = C.sb([128, 172, 2], F32)
    cwb, halob = Buf(), [Buf() for _ in range(172)]
    S.dma("sync", lambda e: e.dma_start(out=CW[:, :, :], in_=cw_l), writes=[cwb])
    S.dma("sync", lambda e: e.dma_start(out=CB[:, :], in_=cb_l), writes=[cwb])
    S.op("pool", lambda e: e.memset(HALO[:, :, :], 0.0), writes=halob)
    ub = [C.sb([128, NT + 2], F32) for _ in range(2)]
    ubb = [Buf() for _ in range(2)]
    cb_ = [C.sb([128, NT], F32) for _ in range(2)]
    cbb = [Buf() for _ in range(2)]
    sg = C.sb([128, NT], F32)
    sgb = Buf()
    xr = [C.sb([128, NT], F32) for _ in range(2)]
    xrb = [Buf() for _ in range(2)]
    hob = [Buf() for _ in range(32)]
    groups = []
    load_xb(C, xbf, 32, 0, NT, XB, xbufs)

    def up_epi(j, jl, bg, bv):
        for k, (bank, gidx) in enumerate(((bg, j), (bv, 86 + j))):
            u, ubf = ub[k], ubb[k]
            c, cbf = cb_[k], cbb[k]
            hb_ = halob[gidx]
            pbs = [C.psb[bank], C.psb[bank + 1]]
            S.op("pool", lambda e, u=u, gidx=gidx: e.tensor_copy(out=u[:, 0:2], in_=HALO[:, gidx, :]),
                 reads=[hb_], writes=[ubf])
            S.op("act", lambda e, u=u, bank=bank: e.activation(out=u[:, 2:NT + 2], in_=C.PS[:, bank * 512:bank * 512 + NT],
                                                               func=AF.Identity), reads=pbs, writes=[ubf])
            S.op("pool", lambda e, u=u, gidx=gidx: e.tensor_copy(out=HALO[:, gidx, :], in_=u[:, NT:NT + 2]),
                 reads=[ubf], writes=[hb_])
            S.op("dve", lambda e, u=u, c=c, gidx=gidx: e.tensor_scalar(
                out=c[:, :], in0=u[:, 2:NT + 2], scalar1=CW[:, 2, gidx:gidx + 1], scalar2=CB[:, gidx:gidx + 1],
                op0=ALU.mult, op1=ALU.add), reads=[ubf, cwb], writes=[cbf])
            for kk in (1, 0):
                S.op("dve", lambda e, u=u, c=c, gidx=gidx, kk=kk: e.scalar_tensor_tensor(
                    out=c[:, :], in0=u[:, kk:kk + NT], scalar=CW[:, kk, gidx:gidx + 1], in1=c[:, :],
                    op0=ALU.mult, op1=ALU.add), reads=[ubf, cwb, cbf], writes=[cbf])
        S.op("act", lambda e: e.activation(out=sg[:, :], in_=cb_[0][:, :], func=AF.Silu), reads=[cbb[0]], writes=[sgb])
        S.op("dve", lambda e: e.tensor_tensor(out=H[:, jl, :], in0=sg[:, :], in1=cb_[1][:, :], op=ALU.mult),
             reads=[sgb, cbb[1]], writes=[hbufs[jl]])

    def down_epi(o, bank, t0, part, reload_next):
        t, tb = xr[o % 2], xrb[o % 2]
        pbs = [C.psb[bank], C.psb[bank + 1]]
        psv = C.PS[:, bank * 512:bank * 512 + NT]
        rows = slice(o * 128, (o + 1) * 128)
        if part == 0:
            S.dma("sync", lambda e: e.dma_start(out=t[:, :], in_=xf32[rows, t0:t0 + NT]), writes=[tb])
            S.op("dve", lambda e: e.scalar_tensor_tensor(out=t[:, :], in0=t[:, :], scalar=ALPHA, in1=psv,
                                                         op0=ALU.mult, op1=ALU.add), reads=pbs + [tb], writes=[tb])
        else:
            S.dma("sync", lambda e: e.dma_start(out=t[:, :], in_=hout[rows, t0:t0 + NT]), reads=[hob[o]], writes=[tb])
            S.op("dve", lambda e: e.tensor_tensor(out=t[:, :], in0=t[:, :], in1=psv, op=ALU.add), reads=pbs + [tb], writes=[tb])
        S.dma("sync", lambda e: e.dma_start(out=hout[rows, t0:t0 + NT], in_=t[:, :]), reads=[tb], writes=[hob[o]])
        if reload_next:
            load_xb(C, xbf, 32, t0 + NT, NT, XB, xbufs)

    ntile = T // NT
    for tile in range(ntile):
        t0 = tile * NT
        for part, (g0, ng) in enumerate(PARTS):
            for jl in range(ng):
                j = g0 + jl
                pp = j % 2
                bg, bv = pp * 4, pp * 4 + 2
                groups.append(dict(w=w_up, c0=j * 128, m=128, kc=32, nt=2, banks=[bg, bg + 1],
                                   rhs=lambda c: (XB[:, c, :], xbufs[c]), epi=None))
                groups.append(dict(w=w_up, c0=DFF + j * 128, m=128, kc=32, nt=2, banks=[bv, bv + 1],
                                   rhs=lambda c: (XB[:, c, :], xbufs[c]),
                                   epi=(lambda j=j, jl=jl, bg=bg, bv=bv: up_epi(j, jl, bg, bv))))
            for o in range(32):
                bank = (o % 2) * 2
                reload_next = (o == 0 and part == len(PARTS) - 1 and tile + 1 < ntile)
                groups.append(dict(w=w_down[g0 * 128:(g0 + ng) * 128, :], c0=o * 128, m=128, kc=ng, nt=2, banks=[bank, bank + 1],
                                   rhs=lambda c: (H[:, c, :], hbufs[c]),
                                   epi=(lambda o=o, bank=bank, t0=t0, part=part, rn=reload_next: down_epi(o, bank, t0, part, rn))))
    stream_linear(C, ws, groups)


GELU_NATIVE = False


def phase_inproj1(C, xbf, w_in, cw_l, cb_l, GG, XC, XCb):
    S = C.S
    C.reset()
    XB = C.sb([128, 32, T], BF16)
    xbufs = [Buf() for _ in range(32)]
    load_xb(C, xbf, 32, 0, T, XB, xbufs)
    ws = WStream(C)
    CW = C.sb([128, 4, 32], F32)
    CB = C.sb([128, 32], F32)
    cwb = Buf()
    S.dma("sync", lambda e: e.dma_start(out=CW[:, :, :], in_=cw_l), writes=[cwb])
    S.dma("sync", lambda e: e.dma_start(out=CB[:, :], in_=cb_l), writes=[cwb])
    u = [C.sb([128, T + 3], F32) for _ in range(2)]
    ubf = [Buf() for _ in range(2)]
    for k in range(2):
        S.op("pool", lambda e, k=k: e.memset(u[k][:, 0:3], 0.0), writes=[ubf[k]])
    c_ = [C.sb([128, T], F32) for _ in range(2)]
    cbf = [Buf() for _ in range(2)]
    cb16 = [C.sb([128, T], BF16) for _ in range(2)]
    cb16b = [Buf() for _ in range(2)]
    groups = []
    for gi in range(64):
        banks = [(gi % 2) * 4 + i for i in range(4)]

        def epi(gi=gi, banks=banks):
            par = gi % 2
            psv = C.PS[:, banks[0] * 512:banks[0] * 512 + T]
            pbs = [C.psb[b] for b in banks]
            if gi < 32:
                g = gi
                t, tb = u[par], ubf[par]
                c, cb2 = c_[par], cbf[par]
                if GELU_NATIVE:
                    S.op("act", lambda e: e.activation(out=c[:, :], in_=psv, func=AF.Gelu_apprx_tanh), reads=pbs, writes=[cb2])
                else:
                    S.op("act", lambda e: e.activation(out=t[:, 3:T + 3], in_=psv, func=AF.Square), reads=pbs, writes=[tb])
                    S.op("dve", lambda e: e.tensor_scalar(out=t[:, 3:T + 3], in0=t[:, 3:T + 3], scalar1=0.044715, scalar2=1.0,
                                                          op0=ALU.mult, op1=ALU.add), reads=[tb], writes=[tb])
                    S.op("dve", lambda e: e.tensor_tensor(out=t[:, 3:T + 3], in0=t[:, 3:T + 3], in1=psv, op=ALU.mult),
                         reads=[tb] + pbs, writes=[tb])
                    S.op("act", lambda e: e.activation(out=t[:, 3:T + 3], in_=t[:, 3:T + 3], func=AF.Sigmoid,
                                                       scale=1.5957691216057308), reads=[tb], writes=[tb])
                    S.op("dve", lambda e: e.tensor_tensor(out=c[:, :], in0=t[:, 3:T + 3], in1=psv, op=ALU.mult),
                         reads=[tb] + pbs, writes=[cb2])
                S.dma("sync", lambda e: e.dma_start(out=GG[g * 128:(g + 1) * 128, :], in_=c[:, :]), reads=[cb2])
            else:
                g = gi - 32
                t, tb = u[par], ubf[par]
                c, cb2 = c_[par], cbf[par]
                S.op("act", lambda e: e.activation(out=t[:, 3:T + 3], in_=psv, func=AF.Identity), reads=pbs, writes=[tb])
                S.op("dve", lambda e: e.tensor_scalar(out=c[:, :], in0=t[:, 3:T + 3], scalar1=CW[:, 3, g:g + 1],
                                                      scalar2=CB[:, g:g + 1], op0=ALU.mult, op1=ALU.add),
                     reads=[tb, cwb], writes=[cb2])
                for kk in (2, 1, 0):
                    S.op("dve", lambda e, kk=kk: e.scalar_tensor_tensor(out=c[:, :], in0=t[:, kk:kk + T], scalar=CW[:, kk, g:g + 1],
                                                                        in1=c[:, :], op0=ALU.mult, op1=ALU.add),
                         reads=[tb, cwb, cb2], writes=[cb2])
                S.dma("sync", lambda e: e.dma_start(out=XC[g * 128:(g + 1) * 128, :], in_=c[:, :]), reads=[cb2])
                y, yb_ = cb16[par], cb16b[par]
                S.op("act", lambda e: e.activation(out=y[:, :], in_=c[:, :], func=AF.Identity), reads=[cb2], writes=[yb_])
                S.dma("sync", lambda e: e.dma_start(out=XCb[g * 128:(g + 1) * 128, :], in_=y[:, :]), reads=[yb_])
        groups.append(dict(w=w_in, c0=gi * 128, m=128, kc=32, nt=4, banks=banks,
                           rhs=lambda c: (XB[:, c, :], xbufs[c]), epi=epi))
    stream_linear(C, ws, groups)


def phase_rglru(C, XC, XCb, GG, wr, wi, rb_l, ib_l, lam_l, YL):
    S = C.S
    C.reset()
    PB = C.sb([128, 128], F32)
    pbb = Buf()
    S.dma("sync", lambda e: e.dma_start(out=PB[:, 0:32], in_=rb_l), writes=[pbb])
    S.dma("sync", lambda e: e.dma_start(out=PB[:, 32:64], in_=ib_l), writes=[pbb])
    S.dma("sync", lambda e: e.dma_start(out=PB[:, 64:96], in_=lam_l), writes=[pbb])
    S.op("act", lambda e: e.activation(out=PB[:, 64:96], in_=PB[:, 64:96], func=AF.Exp, scale=-1.0), reads=[pbb], writes=[pbb])
    S.op("act", lambda e: e.activation(out=PB[:, 64:96], in_=PB[:, 64:96], func=AF.Ln, bias=1.0), reads=[pbb], writes=[pbb])
    S.op("dve", lambda e: e.tensor_scalar(out=PB[:, 64:96], in0=PB[:, 64:96], scalar1=-8.0, scalar2=None, op0=ALU.mult),
         reads=[pbb], writes=[pbb])
    S.op("dve", lambda e: e.tensor_scalar(out=PB[:, 96:128], in0=PB[:, 64:96], scalar1=2.0, scalar2=None, op0=ALU.mult),
         reads=[pbb], writes=[pbb])
    wst = [C.sb([128, 2, 2, 256], F32) for _ in range(2)]
    wstb = [Buf() for _ in range(2)]
    wbf = [C.sb([128, 2, 2, 256], BF16) for _ in range(2)]
    wbfb = [Buf() for _ in range(2)]
    xcb = [C.sb([128, 2, T], BF16) for _ in range(2)]
    xcbb = [Buf() for _ in range(2)]
    NB = 2
    xc = [C.sb([128, T], F32) for _ in range(NB)]
    gg = [C.sb([128, T], F32) for _ in range(NB)]
    R_ = [C.sb([128, T], F32) for _ in range(NB)]
    I_ = [C.sb([128, T], F32) for _ in range(NB)]
    A_ = [C.sb([128, T], F32) for _ in range(NB)]
    M_ = [C.sb([128, T], F32) for _ in range(NB)]
    Y_ = [C.sb([128, T], BF16) for _ in range(NB)]
    bufs = [[Buf() for _ in range(7)] for _ in range(NB)]
    for h in range(16):
        hp = h % 2
        S.dma("sync", lambda e, h=h, hp=hp: e.dma_start(out=wst[hp][:, 0, :, :],
                                                       in_=wr[h].rearrange("(c p) j -> p c j", p=128)), writes=[wstb[hp]])
        S.dma("sync", lambda e, h=h, hp=hp: e.dma_start(out=wst[hp][:, 1, :, :],
                                                       in_=wi[h].rearrange("(c p) j -> p c j", p=128)), writes=[wstb[hp]])
        S.op("pool", lambda e, hp=hp: e.tensor_copy(out=wbf[hp][:, :, :, :], in_=wst[hp][:, :, :, :]),
             reads=[wstb[hp]], writes=[wbfb[hp]])
        S.dma("sync", lambda e, h=h, hp=hp: e.dma_start(
            out=xcb[hp][:, :, :], in_=XCb[h * 256:(h + 1) * 256, :].rearrange("(c p) t -> p c t", p=128)), writes=[xcbb[hp]])
        for jg in range(2):
            g = 2 * h + jg
            k = g % NB
            bx, bg, br, bi, ba, bm, by = bufs[k]
            S.dma("sync", lambda e, g=g, k=k: e.dma_start(out=xc[k][:, :], in_=XC[g * 128:(g + 1) * 128, :]), writes=[bx])
            S.dma("sync", lambda e, g=g, k=k: e.dma_start(out=gg[k][:, :], in_=GG[g * 128:(g + 1) * 128, :]), writes=[bg])
            for gi_ in range(2):
                for ic in range(2):
                    for tt in range(4):
                        b = gi_ * 4 + tt
                        S.op("pe", lambda e, hp=hp, gi_=gi_, ic=ic, tt=tt, b=b, jg=jg: e.matmul(
                            C.bank(b), lhsT=wbf[hp][:, gi_, ic, jg * 128:(jg + 1) * 128],
                            rhs=xcb[hp][:, ic, tt * 512:(tt + 1) * 512], start=(ic == 0), stop=(ic == 1)),
                            reads=[wbfb[hp], xcbb[hp]], writes=[C.psb[b]])
            R, I2, A, M, Y = R_[k], I_[k], A_[k], M_[k], Y_[k]
            S.op("act", lambda e, R=R, g=g: e.activation(out=R[:, :], in_=C.PS[:, 0:T], func=AF.Sigmoid, bias=PB[:, g:g + 1]),
                 reads=C.psb[0:4] + [pbb], writes=[br])
            S.op("act", lambda e, I2=I2, g=g: e.activation(out=I2[:, :], in_=C.PS[:, T:2 * T], func=AF.Sigmoid,
                                                           bias=PB[:, 32 + g:33 + g]), reads=C.psb[4:8] + [pbb], writes=[bi])
            S.op("act", lambda e, A=A, R=R, g=g: e.activation(out=A[:, :], in_=R[:, :], func=AF.Exp, scale=PB[:, 64 + g:65 + g]),
                 reads=[br, pbb], writes=[ba])
            S.op("act", lambda e, M=M, R=R, g=g: e.activation(out=M[:, :], in_=R[:, :], func=AF.Exp, scale=PB[:, 96 + g:97 + g]),
                 reads=[br, pbb], writes=[bm])
            S.op("act", lambda e, M=M: e.activation(out=M[:, :], in_=M[:, :], func=AF.Sqrt, scale=-1.0, bias=1.0),
                 reads=[bm], writes=[bm])
            S.op("pool", lambda e, M=M: e.memset(M[:, 0:1], 1.0), reads=[bm], writes=[bm])
            S.op("pool", lambda e, I2=I2, k=k: e.tensor_tensor(out=I2[:, :], in0=I2[:, :], in1=xc[k][:, :], op=ALU.mult),
                 reads=[bi, bx], writes=[bi])
            S.op("dve", lambda e, I2=I2, M=M: e.tensor_tensor(out=I2[:, :], in0=I2[:, :], in1=M[:, :], op=ALU.mult),
                 reads=[bi, bm], writes=[bi])
            S.op("dve", lambda e, R=R, A=A, I2=I2: e.tensor_tensor_scan(out=R[:, :], data0=A[:, :], data1=I2[:, :], initial=0.0,
                                                                         op0=ALU.mult, op1=ALU.add),
                 reads=[ba, bi, br], writes=[br])
            S.op("dve", lambda e, Y=Y, R=R, k=k: e.tensor_tensor(out=Y[:, :], in0=R[:, :], in1=gg[k][:, :], op=ALU.mult),
                 reads=[br, bg], writes=[by])
            S.dma("sync", lambda e, Y=Y, g=g: e.dma_start(out=YL[g * 128:(g + 1) * 128, :], in_=Y[:, :]), reads=[by])


def make_mask(C, shape, dtype, step, cm, cmp, base=0, rep=1):
    S = C.S
    P, N = shape
    t = C.sb([P, rep * N], dtype)
    b = Buf()
    S.op("pool", lambda e: e.memset(t[:, :], 1.0), writes=[b])
    pat = [[step, N]] if rep == 1 else [[0, rep], [step, N]]
    S.op("pool", lambda e: e.affine_select(out=t[:, :], in_=t[:, :], pattern=pat, compare_op=cmp, fill=0.0, base=base,
                                           channel_multiplier=cm), reads=[b], writes=[b])
    return t, b


def phase_sba(C, pT, YAB):
    S = C.S
    C.reset()
    IDENT, idb = make_mask(C, [128, 128], F32, 1, -1, ALU.is_equal)
    TRI, trb = make_mask(C, [128, 128], F32, -1, 1, ALU.is_ge)
    MSK, mkb = make_mask(C, [128, 128], F32, 1, -1, ALU.is_gt)
    MSKB, mkbb = make_mask(C, [128, 128], BF16, 1, -1, ALU.is_gt)
    ONES = C.sb([128, 128], F32)
    onb = Buf()
    S.op("pool", lambda e: e.memset(ONES[:, :], 1.0), writes=[onb])
    ZW = C.sb([128, 64], BF16)
    ZR = C.sb([128, 512], BF16)
    zb_ = Buf()
    S.op("pool", lambda e: e.memset(ZW[:, :], 0.0), writes=[zb_])
    S.op("pool", lambda e: e.memset(ZR[:, :], 0.0), writes=[zb_])
    stg = [C.sb([128, T], F32) for _ in range(3)]
    stgb = [Buf() for _ in range(3)]
    QB = C.sb([128, T], BF16)
    KB = C.sb([128, T], BF16)
    qbb, kbb = Buf(), Buf()
    VT = C.sb([128, 16, 128], BF16)
    vtb = Buf()
    Rt = [C.sb([128, T], F32) for _ in range(2)]
    Rb = [Buf() for _ in range(2)]
    NP = 16
    Eb = [C.sb([128, 512], F32) for _ in range(NP)]
    Ebb = [Buf() for _ in range(NP)]
    Zs = [C.sb([128, 512], F32) for _ in range(NP)]
    Zsb = [Buf() for _ in range(NP)]
    At = [C.sb([128, 512], BF16) for _ in range(NP)]
    Atb = [Buf() for _ in range(NP)]
    OUT = C.sb([128, T], BF16)
    outb = Buf()
    Dmy = [Buf() for _ in range(NP)]
    pc = 0
    import os
    STG = int(os.environ.get('SBA_STAGE', '9'))
    for hp in range(int(os.environ.get('SBA_HP', '16'))):
        r0 = RPROJ + hp * 128
        for i, off in enumerate((0, 2048, 4096)):
            S.dma("sync", lambda e, i=i, off=off, r0=r0: e.dma_start(out=stg[i][:, :], in_=pT[r0 + off:r0 + off + 128, :]),
                  writes=[stgb[i]])
        S.op("dve", lambda e: e.tensor_copy(out=QB[:, :], in_=stg[0][:, :]), reads=[stgb[0]], writes=[qbb])
        S.op("pool", lambda e: e.tensor_copy(out=KB[:, :], in_=stg[1][:, :]), reads=[stgb[1]], writes=[kbb])
        for jq in range(4):
            for jj in range(4):
                J = jq * 4 + jj
                S.op("pe", lambda e, J=J, jj=jj: e.transpose(out=C.PS[:, jj * 128:(jj + 1) * 128],
                                                            in_=stg[2][:, J * 128:(J + 1) * 128], identity=IDENT[:, :]),
                     reads=[stgb[2], idb], writes=[C.psb[0]])
            S.op("act", lambda e, jq=jq: e.activation(out=VT[:, jq * 4:(jq + 1) * 4, :], in_=C.PS[:, 0:512].rearrange("p (a b) -> p a b", a=4),
                                                      func=AF.Identity), reads=[C.psb[0]], writes=[vtb])
        for hh in range(2):
            S.op("pool", lambda e, hh=hh: e.memset(Rt[hh][:, :], 0.0), writes=[Rb[hh]])
            for b in range(4):
                S.op("pe", lambda e, hh=hh, b=b: e.matmul(C.PS[hh * 64:(hh + 1) * 64, (4 + b) * 512:(5 + b) * 512],
                                                          lhsT=ZW[:, :], rhs=ZR[:, :], start=True, stop=False),
                     reads=[zb_], writes=[C.psb[4 + b]])
        for J in range(15, -1, -1):
            t0 = J * 128
            plist = []
            c0 = t0
            while c0 < T:
                c1 = min((c0 // 512 + 1) * 512, T)
                for hh in range(2):
                    plist.append((c0, c1 - c0, hh, pc % NP, pc % 2))
                    pc += 1
                c0 = c1
            for (c0, n, hh, k, zbank) in plist:
                pb = hh * 64
                E, Eb_, Z, Zb_ = Eb[k], Ebb[k], Zs[k], Zsb[k]
                R, Rb_ = Rt[hh], Rb[hh]
                zps = C.PS[:, zbank * 512:zbank * 512 + n]
                S.op("pe", lambda e, pb=pb, t0=t0, c0=c0, n=n, zps=zps: e.matmul(
                    zps, lhsT=KB[pb:pb + 64, t0:t0 + 128], rhs=QB[pb:pb + 64, c0:c0 + n], start=True, stop=True),
                    reads=[kbb, qbb], writes=[C.psb[zbank]])
                S.op("act", lambda e, E=E, n=n, zps=zps: e.activation(out=E[:, 0:n], in_=zps, func=AF.Exp, scale=0.125),
                     reads=[C.psb[zbank]], writes=[Eb_])
                S.op("dve", lambda e, Z=Z, n=n, zps=zps, R=R, c0=c0: e.scalar_tensor_tensor(
                    out=Z[:, 0:n], in0=zps, scalar=0.125, in1=R[:, c0:c0 + n], op0=ALU.mult, op1=ALU.subtract),
                    reads=[C.psb[zbank], Rb_], writes=[Zb_])
            for (c0, n, hh, k, zbank) in plist:
                E, Eb_, Z, Zb_ = Eb[k], Ebb[k], Zs[k], Zsb[k]
                R, Rb_ = Rt[hh], Rb[hh]
                S.op("act", lambda e, E=E, n=n: e.activation(out=E[:, 0:n], in_=E[:, 0:n], func=AF.Ln, bias=1.0),
                     reads=[Eb_], writes=[Eb_])
                if c0 == t0:
                    S.op("pool", lambda e, E=E: e.tensor_tensor(out=E[:, 0:128], in0=E[:, 0:128], in1=MSK[:, :], op=ALU.mult),
                         reads=[Eb_, mkb], writes=[Eb_])
                S.op("pe", lambda e, E=E, n=n: e.matmul(C.PS[:, 1024:1024 + n], lhsT=TRI[:, :], rhs=E[:, 0:n], start=True, stop=True),
                     reads=[trb, Eb_], writes=[C.psb[2]])
                if J > 0:
                    S.op("pe", lambda e, E=E, n=n: e.matmul(C.PS[:, 1536:1536 + n], lhsT=ONES[:, :], rhs=E[:, 0:n], start=True, stop=True),
                         reads=[onb, Eb_], writes=[C.psb[3]])
                S.op("dve", lambda e, Z=Z, n=n: e.tensor_tensor(out=Z[:, 0:n], in0=Z[:, 0:n], in1=C.PS[:, 1024:1024 + n], op=ALU.subtract),
                     reads=[Zb_, C.psb[2]], writes=[Zb_])
                if J > 0:
                    S.op("dve", lambda e, R=R, c0=c0, n=n: e.tensor_tensor(out=R[:, c0:c0 + n], in0=R[:, c0:c0 + n],
                                                                            in1=C.PS[:, 1536:1536 + n], op=ALU.add),
                         reads=[Rb_, C.psb[3]], writes=[Rb_])
            for (c0, n, hh, k, zbank) in plist:
                pb = hh * 64
                Z, Zb_, A, Ab_ = Zs[k], Zsb[k], At[k], Atb[k]
                S.op("act", lambda e, A=A, Z=Z, n=n: e.activation(out=A[:, 0:n], in_=Z[:, 0:n], func=AF.Exp),
                     reads=[Zb_], writes=[Ab_])
                if c0 == t0:
                    S.op("pool", lambda e, A=A: e.tensor_tensor(out=A[:, 0:128], in0=A[:, 0:128], in1=MSKB[:, :], op=ALU.mult),
                         reads=[Ab_, mkbb], writes=[Ab_])
                ob = 4 + c0 // 512
                S.op("pe", lambda e, pb=pb, hh=hh, J=J, A=A, n=n, c0=c0: e.matmul(
                    C.PS[pb:pb + 64, 2048 + c0:2048 + c0 + n], lhsT=VT[:, J, hh * 64:(hh + 1) * 64], rhs=A[:, 0:n],
                    start=False, stop=(J == 0)), reads=[vtb, Ab_], writes=[C.psb[ob]])
        S.op("act", lambda e: e.activation(out=OUT[:, :], in_=C.PS[:, 2048:4096], func=AF.Identity),
             reads=C.psb[4:8], writes=[outb])
        S.dma("sync", lambda e, hp=hp: e.dma_start(out=YAB[RW + hp * 128:RW + (hp + 1) * 128, :], in_=OUT[:, :]), reads=[outb])


C0 = float(np.exp(-0.5))
CH = 64
NCH = T // CH


def phase_rwkv_prep(C, pT, du, iu, gu, vecs, KKt, Rtl, Kh, Bh, GT, BV, GC):
    S = C.S
    C.reset()
    DU = C.sb([96, RW], F32)
    IU = C.sb([96, RW], F32)
    GU = C.sb([128, 2, RW], F32)
    lb = Buf()
    S.dma("sync", lambda e: e.dma_start(out=DU[:, :], in_=du), writes=[lb])
    S.dma("sync", lambda e: e.dma_start(out=IU[:, :], in_=iu), writes=[lb])
    S.dma("sync", lambda e: e.dma_start(out=GU[:, :, :], in_=gu.rearrange("(c p) n -> p c n", p=128)), writes=[lb])
    VEC = C.sb([128, 7, 16], F32)
    vb = Buf()
    S.dma("sync", lambda e: e.dma_start(out=VEC[:, :, :], in_=vecs), writes=[vb])
    S.op("dve", lambda e: e.tensor_scalar(out=VEC[:, 5, :], in0=VEC[:, 3, :], scalar1=-1.0, scalar2=1.0, op0=ALU.mult, op1=ALU.add),
         reads=[vb], writes=[vb])
    TW = C.sb([96, T], F32)
    AD = C.sb([96, T], F32)
    SG = C.sb([128, 2, T], F32)
    twb, adb, sgb = Buf(), Buf(), Buf()
    S.dma("sync", lambda e: e.dma_start(out=TW[:, :], in_=pT[6144:6240, :]), writes=[twb])
    S.dma("sync", lambda e: e.dma_start(out=AD[:, :], in_=pT[6240:6336, :]), writes=[adb])
    S.dma("sync", lambda e: e.dma_start(out=SG[:, :, :], in_=pT[6336:6592, :].rearrange("(c p) t -> p c t", p=128)), writes=[sgb])
    S.op("act", lambda e: e.activation(out=TW[:, :], in_=TW[:, :], func=AF.Tanh), reads=[twb], writes=[twb])
    S.op("act", lambda e: e.activation(out=SG[:, :, :], in_=SG[:, :, :], func=AF.Sigmoid), reads=[sgb], writes=[sgb])
    BLK = C.sb([128, 128], F32)
    blb = Buf()
    S.op("pool", lambda e: e.memset(BLK[:, :], 0.0), writes=[blb])
    S.op("pool", lambda e: e.memset(BLK[0:64, 0:64], 1.0), reads=[blb], writes=[blb])
    S.op("pool", lambda e: e.memset(BLK[64:128, 64:128], 1.0), reads=[blb], writes=[blb])
    RM = C.sb([128, T], F32)
    rmb = Buf()
    S.op("pool", lambda e: e.memset(RM[:, :], 1.0), writes=[rmb])
    S.op("pool", lambda e: e.memset(RM[:, :].rearrange("p (c n) -> p c n", n=CH)[:, :, 0:1], 0.0), reads=[rmb], writes=[rmb])
    X = [C.sb([128, T], F32) for _ in range(11)]
    xb = [Buf() for _ in range(11)]
    X0b = [C.sb([128, T], F32) for _ in range(3)]
    x0bb = [Buf() for _ in range(3)]
    GCS = [C.sb([128, NCH], F32) for _ in range(2)]
    gcsbb = [Buf() for _ in range(2)]
    PSa, PSb = C.PS[:, 0:T], C.PS[:, T:2 * T]
    pa, pb_ = C.psb[0:4], C.psb[4:8]
    for j in range(16):
        if j % 2 == 0:
            Rr, Kk, Vv, br, bk, bv = X[0], X[1], X[2], xb[0], xb[1], xb[2]
        else:
            Rr, Kk, Vv, br, bk, bv = X0b[0], X0b[1], X0b[2], x0bb[0], x0bb[1], x0bb[2]
        fs = slice(j * 128, (j + 1) * 128)
        S.dma("sync", lambda e, Rr=Rr, fs=fs: e.dma_start(out=Rr[:, :], in_=pT[fs, :]), writes=[br])
        S.dma("sync", lambda e, Kk=Kk, j=j: e.dma_start(out=Kk[:, :], in_=pT[2048 + j * 128:2048 + (j + 1) * 128, :]), writes=[bk])
        S.dma("sync", lambda e, Vv=Vv, j=j: e.dma_start(out=Vv[:, :], in_=pT[4096 + j * 128:4096 + (j + 1) * 128, :]), writes=[bv])
        vec = lambda i, j=j: VEC[:, i, j:j + 1]
        for tt in range(4):
            ts_ = slice(tt * 512, (tt + 1) * 512)
            S.op("pe", lambda e, tt=tt, ts_=ts_, fs=fs: e.matmul(C.bank(tt), lhsT=DU[:, fs], rhs=TW[:, ts_], start=True, stop=True),
                 reads=[lb, twb], writes=[C.psb[tt]])
        for tt in range(4):
            ts_ = slice(tt * 512, (tt + 1) * 512)
            S.op("pe", lambda e, tt=tt, ts_=ts_, fs=fs: e.matmul(C.bank(4 + tt), lhsT=IU[:, fs], rhs=AD[:, ts_], start=True, stop=True),
                 reads=[lb, adb], writes=[C.psb[4 + tt]])
        S.op("act", lambda e, vec=vec: e.activation(out=X[3][:, :], in_=PSa, func=AF.Sigmoid, bias=vec(0)), reads=pa + [vb], writes=[xb[3]])
        S.op("act", lambda e, vec=vec: e.activation(out=X[4][:, :], in_=PSb, func=AF.Sigmoid, bias=vec(1)), reads=pb_ + [vb], writes=[xb[4]])
        S.op("dve", lambda e: e.tensor_tensor_scan(out=X[5][:, :], data0=RM[:, :], data1=X[3][:, :], initial=0.0, op0=ALU.mult, op1=ALU.add),
             reads=[rmb, xb[3]], writes=[xb[5]])
        S.op("act", lambda e: e.activation(out=X[6][:, :], in_=X[5][:, :], func=AF.Exp, scale=-C0), reads=[xb[5]], writes=[xb[6]])
        S.op("act", lambda e: e.activation(out=X[7][:, :], in_=X[5][:, :], func=AF.Exp, scale=C0), reads=[xb[5]], writes=[xb[7]])
        S.op("pool", lambda e: e.tensor_tensor(out=X[3][:, :], in0=X[5][:, :], in1=X[3][:, :], op=ALU.subtract), reads=[xb[5], xb[3]], writes=[xb[3]])
        S.op("act", lambda e: e.activation(out=X[8][:, :], in_=X[3][:, :], func=AF.Exp, scale=-C0), reads=[xb[3]], writes=[xb[8]])
        gcs, gcsb = GCS[j % 2], gcsbb[j % 2]
        S.op("pool", lambda e, gcs=gcs: e.tensor_copy(out=gcs[:, :], in_=X[6][:, :].rearrange("p (c n) -> p c n", n=CH)[:, :, CH - 1]),
             reads=[xb[6]], writes=[gcsb])
        S.dma("sync", lambda e, fs=fs, gcs=gcs: e.dma_start(out=GC[fs, :], in_=gcs[:, :]), reads=[gcsb])
        S.op("act", lambda e, Kk=Kk, vec=vec: e.activation(out=X[9][:, :], in_=Kk[:, :], func=AF.Identity, scale=vec(2)),
             reads=[bk, vb], writes=[xb[9]])
        S.op("act", lambda e: e.activation(out=X[10][:, :], in_=X[9][:, :], func=AF.Square), reads=[xb[9]], writes=[xb[10]])
        for tt in range(4):
            ts_ = slice(tt * 512, (tt + 1) * 512)
            S.op("pe", lambda e, tt=tt, ts_=ts_: e.matmul(C.bank(tt), lhsT=BLK[:, :], rhs=X[10][:, ts_], start=True, stop=True),
                 reads=[blb, xb[10]], writes=[C.psb[tt]])
        S.op("dve", lambda e: e.tensor_scalar(out=X[10][:, :], in0=PSa, scalar1=1e-24, scalar2=None, op0=ALU.max), reads=pa + [xb[10]], writes=[xb[10]])
        S.op("act", lambda e: e.activation(out=X[10][:, :], in_=X[10][:, :], func=AF.Sqrt), reads=[xb[10]], writes=[xb[10]])
        S.op("dve", lambda e: e.reciprocal(out=X[10][:, :], in_=X[10][:, :]), reads=[xb[10]], writes=[xb[10]])
        S.op("dve", lambda e: e.tensor_tensor(out=X[9][:, :], in0=X[9][:, :], in1=X[10][:, :], op=ALU.mult), reads=[xb[9], xb[10]], writes=[xb[9]])
        S.op("act", lambda e, vec=vec: e.activation(out=X[10][:, :], in_=X[4][:, :], func=AF.Identity, scale=vec(3), bias=vec(5)),
             reads=[xb[4], vb, xb[10]], writes=[xb[10]])
        S.op("pool", lambda e, Kk=Kk: e.tensor_tensor(out=Kk[:, :], in0=Kk[:, :], in1=X[10][:, :], op=ALU.mult), reads=[bk, xb[10]], writes=[bk])
        S.op("dve", lambda e: e.tensor_tensor(out=X[10][:, :], in0=X[9][:, :], in1=X[4][:, :], op=ALU.mult), reads=[xb[9], xb[4], xb[10]], writes=[xb[10]])
        S.op("pool", lambda e: e.tensor_tensor(out=X[9][:, :], in0=X[9][:, :], in1=X[8][:, :], op=ALU.mult), reads=[xb[9], xb[8], xb[10]], writes=[xb[9]])
        S.dma("sync", lambda e, fs=fs: e.dma_start(out=KKt[fs, :], in_=X[9][:, :]), reads=[xb[9]])
        S.op("dve", lambda e: e.tensor_tensor(out=X[10][:, :], in0=X[10][:, :], in1=X[7][:, :], op=ALU.mult), reads=[xb[10], xb[7]], writes=[xb[10]])
        S.dma("sync", lambda e, fs=fs: e.dma_start(out=Bh[fs, :], in_=X[10][:, :]), reads=[xb[10]])
        S.op("pool", lambda e, Rr=Rr, Kk=Kk: e.tensor_tensor(out=X[3][:, :], in0=Rr[:, :], in1=Kk[:, :], op=ALU.mult), reads=[br, bk, xb[3]], writes=[xb[3]])
        S.op("act", lambda e, vec=vec: e.activation(out=X[3][:, :], in_=X[3][:, :], func=AF.Identity, scale=vec(4)),
             reads=[xb[3], vb], writes=[xb[3]])
        for tt in range(4):
            ts_ = slice(tt * 512, (tt + 1) * 512)
            S.op("pe", lambda e, tt=tt, ts_=ts_: e.matmul(C.bank(4 + tt), lhsT=BLK[:, :], rhs=X[3][:, ts_], start=True, stop=True),
                 reads=[blb, xb[3]], writes=[C.psb[4 + tt]])
        S.op("dve", lambda e, Vv=Vv: e.tensor_tensor(out=X[3][:, :], in0=Vv[:, :], in1=PSb, op=ALU.mult), reads=pb_ + [bv, xb[3]], writes=[xb[3]])
        S.dma("sync", lambda e, fs=fs: e.dma_start(out=BV[fs, :], in_=X[3][:, :]), reads=[xb[3]])
        S.op("pool", lambda e, Rr=Rr: e.tensor_tensor(out=Rr[:, :], in0=Rr[:, :], in1=X[6][:, :], op=ALU.mult), reads=[br, xb[6]], writes=[br])
        S.dma("sync", lambda e, fs=fs, Rr=Rr: e.dma_start(out=Rtl[fs, :], in_=Rr[:, :]), reads=[br])
        S.op("dve", lambda e, Kk=Kk: e.tensor_tensor(out=Kk[:, :], in0=Kk[:, :], in1=X[7][:, :], op=ALU.mult), reads=[bk, xb[7]], writes=[bk])
        S.dma("sync", lambda e, fs=fs, Kk=Kk: e.dma_start(out=Kh[fs, :], in_=Kk[:, :]), reads=[bk])
        for tt in range(4):
            ts_ = slice(tt * 512, (tt + 1) * 512)
            for c in range(2):
                S.op("pe", lambda e, tt=tt, ts_=ts_, c=c, fs=fs: e.matmul(C.bank(tt), lhsT=GU[:, c, fs], rhs=SG[:, c, ts_], start=(c == 0), stop=(c == 1)),
                     reads=[lb, sgb], writes=[C.psb[tt]])
        S.op("act", lambda e: e.activation(out=X[5][:, :], in_=PSa, func=AF.Identity), reads=pa + [xb[5]], writes=[xb[5]])
        S.dma("sync", lambda e, fs=fs: e.dma_start(out=GT[fs, :], in_=X[5][:, :]), reads=[xb[5]])


def phase_rwkv_core(C, pT, KKt, Rtl, Kh, Bh, GT, BV, GC, lnx_l, YAB):
    import os
    S = C.S
    C.reset()
    P = 64
    SU, sub = make_mask(C, [P, CH], F32, 1, -1, ALU.is_gt, rep=NCH)
    IUm, iub = make_mask(C, [P, CH], F32, 1, -1, ALU.is_ge, rep=NCH)
    SL, slb = make_mask(C, [P, CH], F32, -1, 1, ALU.is_gt, rep=NCH)
    EYE, eyb = make_mask(C, [P, CH], F32, 1, -1, ALU.is_equal, rep=NCH)
    ID = EYE[:, 0:CH]
    ONE = C.sb([P, P], F32)
    oneb = Buf()
    S.op("pool", lambda e: e.memset(ONE[:, :], 1.0), writes=[oneb])
    EPS = C.sb([P, 1], F32)
    S.op("pool", lambda e: e.memset(EPS[:, :], LNX_EPS), writes=[oneb])
    LN = C.sb([P, 2, 32], F32)
    lnb = Buf()
    S.dma("sync", lambda e: e.dma_start(out=LN[:, :, :], in_=lnx_l), writes=[lnb])
    B = [C.sb([P, T], F32) for _ in range(14)]
    bb = [Buf() for _ in range(14)]
    INB = [C.sb([P, T], F32) for _ in range(5)]
    inbb = [Buf() for _ in range(5)]
    GCt = [C.sb([P, NCH], F32) for _ in range(2)]
    gcb = [Buf() for _ in range(2)]
    A = [C.sb([P, P], F32) for _ in range(2)]
    ab = [Buf() for _ in range(2)]
    RH = [C.sb([P, P], F32) for _ in range(2)]
    rhb = [Buf() for _ in range(2)]
    UN = [C.sb([P, P], F32) for _ in range(2)]
    unb = [Buf() for _ in range(2)]
    OUT = C.sb([P, T], BF16)
    outb = Buf()
    PSa, PSb = C.PS[0:P, 0:T], C.PS[0:P, T:2 * T]
    pa, pb_ = C.psb[0:4], C.psb[4:8]
    cs = lambda c: slice(c * CH, (c + 1) * CH)

    def mm_set(dst_ps, banks, lhs, lb_, rhs, rb_):
        for c in range(NCH):
            S.op("pe", lambda e, c=c: e.matmul(dst_ps[:, cs(c)], lhsT=lhs[:, cs(c)], rhs=rhs[:, cs(c)], start=True, stop=True),
                 reads=[lb_, rb_], writes=[banks[c // 8]])

    nheads = int(os.environ.get("RWKV_HEADS", "32"))
    def do_head(h):
        fs = slice(h * P, (h + 1) * P)
        if (h + int(os.environ.get('RWKV_PAR', '0'))) % 2 == 0:
            tin, tinb = B[0:5], bb[0:5]
        else:
            tin, tinb = INB, inbb
        kkt, rt, kh, bh, vv = tin
        bkkt, brt, bkh, bbh, bvv = tinb
        for t_, b_, src in ((kkt, bkkt, KKt[fs, :]), (rt, brt, Rtl[fs, :]), (kh, bkh, Kh[fs, :]), (bh, bbh, Bh[fs, :]),
                            (vv, bvv, pT[4096 + h * P:4096 + (h + 1) * P, :])):
            S.dma("sync", lambda e, t_=t_, src=src: e.dma_start(out=t_[:, :], in_=src), writes=[b_])
        gct, gcbb = GCt[h % 2], gcb[h % 2]
        S.dma("sync", lambda e, gct=gct, fs=fs: e.dma_start(out=gct[:, :], in_=GC[fs, :]), writes=[gcbb])
        khT, bhT, vT = B[5], B[6], B[7]
        for i_, (src, sb_, dst, db_) in enumerate(((kh, bkh, khT, bb[5]), (bh, bbh, bhT, bb[6]), (vv, bvv, vT, bb[7]))):
            ps_, banks = (PSa, pa) if i_ % 2 == 0 else (PSb, pb_)
            for c in range(NCH):
                S.op("pe", lambda e, c=c, ps_=ps_, src=src: e.transpose(out=ps_[:, cs(c)], in_=src[:, cs(c)], identity=ID),
                     reads=[sb_, eyb], writes=[banks[c // 8]])
            S.op("act", lambda e, dst=dst, ps_=ps_: e.activation(out=dst[:, :], in_=ps_, func=AF.Identity), reads=banks, writes=[db_])
        Xm, XTm, MkT, MrkT, MrbT, W1 = B[8], B[9], B[10], B[11], B[12], B[13]
        specs = ((bh, bbh, kkt, bkkt, SU, sub, Xm, bb[8]), (kkt, bkkt, bh, bbh, SL, slb, XTm, bb[9]),
                 (kh, bkh, kkt, bkkt, SU, sub, MkT, bb[10]), (kh, bkh, rt, brt, IUm, iub, MrkT, bb[11]),
                 (bh, bbh, rt, brt, IUm, iub, MrbT, bb[12]))
        for i_, (l_, lb_, r_, rb_, m_, mb_, d_, db_) in enumerate(specs):
            ps_, banks = (PSb, pb_) if i_ % 2 == 0 else (PSa, pa)
            mm_set(ps_, banks, l_, lb_, r_, rb_)
            S.op("dve", lambda e, d_=d_, ps_=ps_, m_=m_: e.tensor_tensor(out=d_[:, :], in0=ps_, in1=m_[:, :], op=ALU.mult),
                 reads=banks + [mb_], writes=[db_])
        mm_set(PSa, pa, MkT, bb[10], vT, bb[7])
        S.op("act", lambda e: e.activation(out=W1[:, :], in_=PSa, func=AF.Identity), reads=pa, writes=[bb[13]])
        G, gb_ = vv, bvv
        S.op("dve", lambda e, G=G: e.tensor_tensor(out=G[:, :], in0=EYE[:, :], in1=Xm[:, :], op=ALU.subtract),
             reads=[eyb, bb[8], bvv], writes=[gb_])
        Pc, Pcb, PTc, PTcb = Xm, bb[8], XTm, bb[9]
        Pn, Pnb, PTn, PTnb = kh, bkh, bh, bbh
        for r in range(5):
            mm_set(PSb, pb_, Pc, Pcb, PTc, PTcb)
            if r < 4:
                mm_set(PSa, pa, PTc, PTcb, Pc, Pcb)
            S.op("act", lambda e, PTn=PTn: e.activation(out=PTn[:, :], in_=PSb, func=AF.Identity), reads=pb_, writes=[PTnb])
            if r < 4:
                S.op("dve", lambda e, Pn=Pn: e.tensor_copy(out=Pn[:, :], in_=PSa), reads=pa, writes=[Pnb])
            mm_set(PSb, pb_, PTn, PTnb, G, gb_)
            S.op("dve", lambda e, G=G: e.tensor_tensor(out=G[:, :], in0=G[:, :], in1=PSb, op=ALU.add), reads=pb_ + [gb_], writes=[gb_])
            Pc, Pcb, PTc, PTcb, Pn, Pnb, PTn, PTnb = Pn, Pnb, PTn, PTnb, Pc, Pcb, PTc, PTcb
        for c in range(NCH):
            S.op("pe", lambda e, c=c: e.transpose(out=PSa[:, cs(c)], in_=kkt[:, cs(c)], identity=ID),
                 reads=[bkkt, eyb], writes=[pa[c // 8]])
        S.op("act", lambda e: e.activation(out=Xm[:, :], in_=PSa, func=AF.Identity), reads=pa, writes=[bb[8]])
        mm_set(PSb, pb_, Xm, bb[8], G, gb_)
        S.op("dve", lambda e: e.tensor_copy(out=XTm[:, :], in_=PSb), reads=pb_, writes=[bb[9]])
        mm_set(PSa, pa, G, gb_, W1, bb[13])
        S.op("act", lambda e: e.activation(out=W1[:, :], in_=PSa, func=AF.Identity, scale=-1.0), reads=pa, writes=[bb[13]])
        Y, yb_ = MkT, bb[10]
        S.op("pool", lambda e: e.memset(A[0][:, :], 0.0), writes=[ab[0]])
        for c in range(0 if not os.environ.get('RWKV_SKIPREC') else NCH, NCH):
            a0, a0b, a1, a1b = A[c % 2], ab[c % 2], A[(c + 1) % 2], ab[(c + 1) % 2]
            rh, rhb_, un, unb_ = RH[c % 2], rhb[c % 2], UN[c % 2], unb[c % 2]
            k0 = (c % 2) * 4
            ps_rhs = C.PS[0:P, (k0 + 0) * 512:(k0 + 0) * 512 + P]
            ps_u = C.PS[0:P, (k0 + 1) * 512:(k0 + 1) * 512 + P]
            ps_s = C.PS[0:P, (k0 + 2) * 512:(k0 + 2) * 512 + P]
            ps_y = C.PS[0:P, (k0 + 3) * 512:(k0 + 3) * 512 + P]
            b_rhs, b_u, b_s, b_y = C.psb[k0], C.psb[k0 + 1], C.psb[k0 + 2], C.psb[k0 + 3]
            S.op("pe", lambda e, c=c, a0=a0, ps_u=ps_u: e.matmul(ps_u, lhsT=XTm[:, cs(c)], rhs=a0[:, :], start=True, stop=True),
                 reads=[bb[9], a0b], writes=[b_u])
            S.op("pe", lambda e, c=c, ps_s=ps_s: e.matmul(ps_s, lhsT=khT[:, cs(c)], rhs=vT[:, cs(c)], start=True, stop=False),
                 reads=[bb[5], bb[7]], writes=[b_s])
            S.op("pe", lambda e, a0=a0, ps_s=ps_s: e.matmul(ps_s, lhsT=ID, rhs=a0[:, :], start=False, stop=False),
                 reads=[eyb, a0b], writes=[b_s])
            S.op("pe", lambda e, c=c, a0=a0, ps_y=ps_y: e.matmul(ps_y, lhsT=a0[:, :], rhs=rt[:, cs(c)], start=True, stop=False),
                 reads=[a0b, brt], writes=[b_y])
            S.op("pe", lambda e, c=c, ps_y=ps_y: e.matmul(ps_y, lhsT=vT[:, cs(c)], rhs=MrkT[:, cs(c)], start=False, stop=False),
                 reads=[bb[7], bb[11]], writes=[b_y])
            S.op("dve", lambda e, c=c, un=un, ps_u=ps_u: e.tensor_tensor(out=un[:, :], in0=W1[:, cs(c)], in1=ps_u, op=ALU.subtract),
                 reads=[b_u, bb[13]], writes=[unb_])
            S.op("pe", lambda e, c=c, un=un, ps_s=ps_s: e.matmul(ps_s, lhsT=bhT[:, cs(c)], rhs=un[:, :], start=False, stop=True),
                 reads=[bb[6], unb_], writes=[b_s])
            S.op("pe", lambda e, c=c, un=un, ps_y=ps_y: e.matmul(ps_y, lhsT=un[:, :], rhs=MrbT[:, cs(c)], start=False, stop=True),
                 reads=[unb_, bb[12]], writes=[b_y])
            S.op("dve", lambda e, c=c, a1=a1, ps_s=ps_s, gct=gct: e.tensor_scalar(out=a1[:, :], in0=ps_s, scalar1=gct[:, c:c + 1], scalar2=None, op0=ALU.mult),
                 reads=[b_s, gcbb], writes=[a1b])
            S.op("act", lambda e, c=c, ps_y=ps_y: e.activation(out=Y[:, cs(c)], in_=ps_y, func=AF.Identity), reads=[b_y], writes=[yb_])
        if h == 0 and "rdbg" in C.debug_out:
            rdbg = C.dram("rdbg", [P, 10, T])
            for i_, (t_, b_) in enumerate(((G, gb_), (W1, bb[13]), (MrkT, bb[11]), (MrbT, bb[12]), (khT, bb[5]), (bhT, bb[6]),
                                           (vT, bb[7]), (Y, yb_), (kkt, bkkt), (rt, brt))):
                S.dma("sync", lambda e, i_=i_, t_=t_: e.dma_start(out=rdbg[:, i_, :], in_=t_[:, :]), reads=[b_])
        SQ, sqb, MEAN, mnb = kh, bkh, bh, bbh
        S.dma("sync", lambda e, fs=fs: e.dma_start(out=Xm[:, :], in_=BV[fs, :]), writes=[bb[8]])
        S.dma("sync", lambda e, fs=fs: e.dma_start(out=XTm[:, :], in_=GT[fs, :]), writes=[bb[9]])
        S.op("act", lambda e: e.activation(out=SQ[:, :], in_=Y[:, :], func=AF.Square), reads=[yb_], writes=[sqb])
        for tt in range(4):
            ts_ = slice(tt * 512, (tt + 1) * 512)
            S.op("pe", lambda e, tt=tt, ts_=ts_: e.matmul(C.bank(tt, P), lhsT=ONE[:, :], rhs=Y[:, ts_], start=True, stop=True),
                 reads=[oneb, yb_], writes=[C.psb[tt]])
        for tt in range(4):
            ts_ = slice(tt * 512, (tt + 1) * 512)
            S.op("pe", lambda e, tt=tt, ts_=ts_: e.matmul(C.bank(4 + tt, P), lhsT=ONE[:, :], rhs=SQ[:, ts_], start=True, stop=True),
                 reads=[oneb, sqb], writes=[C.psb[4 + tt]])
        S.op("act", lambda e: e.activation(out=MEAN[:, :], in_=PSa, func=AF.Identity, scale=1.0 / P), reads=pa, writes=[mnb])
        S.op("dve", lambda e: e.tensor_tensor(out=SQ[:, :], in0=MEAN[:, :], in1=MEAN[:, :], op=ALU.mult), reads=[mnb, sqb], writes=[sqb])
        S.op("dve", lambda e: e.scalar_tensor_tensor(out=SQ[:, :], in0=PSb, scalar=1.0 / P, in1=SQ[:, :], op0=ALU.mult, op1=ALU.subtract),
             reads=pb_ + [sqb], writes=[sqb])
        S.op("act", lambda e: e.activation(out=SQ[:, :], in_=SQ[:, :], func=AF.Sqrt, bias=EPS[:, 0:1]), reads=[sqb, oneb], writes=[sqb])
        S.op("dve", lambda e: e.reciprocal(out=SQ[:, :], in_=SQ[:, :]), reads=[sqb], writes=[sqb])
        S.op("pool", lambda e: e.tensor_tensor(out=Y[:, :], in0=Y[:, :], in1=MEAN[:, :], op=ALU.subtract), reads=[yb_, mnb], writes=[yb_])
        S.op("dve", lambda e: e.tensor_tensor(out=Y[:, :], in0=Y[:, :], in1=SQ[:, :], op=ALU.mult), reads=[yb_, sqb], writes=[yb_])
        S.op("dve", lambda e, h=h: e.tensor_scalar(out=Y[:, :], in0=Y[:, :], scalar1=LN[:, 0, h:h + 1], scalar2=LN[:, 1, h:h + 1],
                                                   op0=ALU.mult, op1=ALU.add), reads=[yb_, lnb], writes=[yb_])
        S.op("pool", lambda e: e.tensor_tensor(out=Y[:, :], in0=Y[:, :], in1=Xm[:, :], op=ALU.add), reads=[yb_, bb[8]], writes=[yb_])
        S.op("dve", lambda e: e.tensor_tensor(out=OUT[:, :], in0=Y[:, :], in1=XTm[:, :], op=ALU.mult), reads=[yb_, bb[9]], writes=[outb])
        S.dma("sync", lambda e, fs=fs: e.dma_start(out=YAB[fs, :], in_=OUT[:, :]), reads=[outb])
        if os.environ.get('RWKV_BAR'):
            S.barrier()

    for h in range(nheads):
        do_head(h)


def vec_l(v, ngroups):
    o = np.zeros(ngroups * 128, np.float32)
    o[:v.shape[0]] = v
    return np.ascontiguousarray(o.reshape(ngroups, 128).T)


def build(phases=None, ext_in=(), debug_out=()):
    nc = bass.Bass("TRN2", target_bir_lowering=False)
    C = Ctx(nc, debug_out)
    C.ext_in = set(ext_in)
    allp = ("inproj0", "rwkv", "sba", "lin0", "ln0a", "ffn0", "ln0b", "inproj1", "rglru", "lin1", "ln1a", "ffn1", "ln1b")
    phases = allp if phases is None else phases
    I = lambda name, shape, dt=F32: C.inp(name, shape, dt)
    Dm = lambda name, shape, dt=F32: (C.inp(name, shape, dt) if name in C.ext_in else C.dram(name, shape, dt))
    xT = I("xT", [D, T])
    if "inproj0" in phases:
        pT = Dm("pT", [W0COLS, T])
        phase_inproj0(C, xT, I("l0_w_in", [D, W0COLS]), I("l0_mu", [128, 52]), pT)
    if "sba" in phases or "rwkv" in phases:
        if "inproj0" not in phases:
            pT = Dm("pT", [W0COLS, T])
        yab = Dm("yab0", [D, T], BF16)
    if "rwkv" in phases:
        sc = {n: Dm(n, [RW, T]) for n in ("KKt", "Rtl", "Kh", "Bh", "GT", "BV")}
        GC = Dm("GC", [RW, NCH])
        phase_rwkv_prep(C, pT, I("l0_decay_up", [96, RW]), I("l0_iclr_up", [96, RW]), I("l0_gate_up", [256, RW]),
                        I("l0_vecs", [128, 7, 16]), sc["KKt"], sc["Rtl"], sc["Kh"], sc["Bh"], sc["GT"], sc["BV"], GC)
        if "rwkv_prep_only" not in phases:
            phase_rwkv_core(C, pT, sc["KKt"], sc["Rtl"], sc["Kh"], sc["Bh"], sc["GT"], sc["BV"], GC, I("l0_lnx", [64, 2, 32]), yab)
    if "sba" in phases:
        phase_sba(C, pT, yab)
    if "lin0" in phases:
        if not ("sba" in phases or "rwkv" in phases):
            yab = Dm("yab0", [D, T], BF16)
        h0a = Dm("h0a", [D, T])
        phase_linear_res(C, yab, I("l0_w_out", [D, D]), xT, h0a)
    if "ln0a" in phases:
        h0a = Dm("h0a", [D, T]) if "lin0" not in phases else h0a
        x1f, x1b = Dm("x1f", [D, T]), Dm("x1b", [D, T], BF16)
        phase_ln(C, h0a, I("l0_ln_mix_g", [128, 32]), I("l0_ln_mix_b", [128, 32]), x1f, x1b)
    if "ffn0" in phases:
        if "ln0a" not in phases:
            x1f, x1b = Dm("x1f", [D, T]), Dm("x1b", [D, T], BF16)
        h0b = Dm("h0b", [D, T])
        phase_ffn(C, x1b, x1f, I("l0_ffn_up", [D, 2 * DFF]), I("l0_ffn_cw", [128, 3, 172]), I("l0_ffn_cb", [128, 172]),
                  I("l0_ffn_down", [DFF, D]), h0b)
    if "ln0b" in phases:
        if "ffn0" not in phases:
            h0b = Dm("h0b", [D, T])
        x2f, x2b = Dm("x2f", [D, T]), Dm("x2b", [D, T], BF16)
        phase_ln(C, h0b, I("l0_ln_ffn_g", [128, 32]), I("l0_ln_ffn_b", [128, 32]), x2f, x2b)
    if "inproj1" in phases:
        if "ln0b" not in phases:
            x2f, x2b = Dm("x2f", [D, T]), Dm("x2b", [D, T], BF16)
        GG, XC, XCb = Dm("GG", [D, T]), Dm("XC", [D, T]), Dm("XCb", [D, T], BF16)
        phase_inproj1(C, x2b, I("l1_w_in", [D, 2 * D]), I("l1_cw", [128, 4, 32]), I("l1_cb", [128, 32]), GG, XC, XCb)
    if "rglru" in phases:
        YL = Dm("YL", [D, T], BF16)
        phase_rglru(C, XC, XCb, GG, I("l1_gate_r_w", [16, 256, 256]), I("l1_gate_i_w", [16, 256, 256]),
                    I("l1_rb", [128, 32]), I("l1_ib", [128, 32]), I("l1_lam", [128, 32]), YL)
    if "lin1" in phases:
        h1a = Dm("h1a", [D, T])
        phase_linear_res(C, YL, I("l1_w_out", [D, D]), x2f, h1a)
    if "ln1a" in phases:
        x3f, x3b = Dm("x3f", [D, T]), Dm("x3b", [D, T], BF16)
        phase_ln(C, h1a, I("l1_ln_mix_g", [128, 32]), I("l1_ln_mix_b", [128, 32]), x3f, x3b)
    if "ffn1" in phases:
        h1b = Dm("h1b", [D, T])
        phase_ffn(C, x3b, x3f, I("l1_ffn_up", [D, 2 * DFF]), I("l1_ffn_cw", [128, 3, 172]), I("l1_ffn_cb", [128, 172]),
                  I("l1_ffn_down", [DFF, D]), h1b)
    if "ln1b" in phases:
        outT = C.nc.dram_tensor("outT", [D, T], F32, kind="ExternalOutput").ap()
        phase_ln(C, h1b, I("l1_ln_ffn_g", [128, 32]), I("l1_ln_ffn_b", [128, 32]), outT, None)
    C.S.barrier()
    C.S.emit([])
    return nc


def prep_inputs(inputs, b):
    m = {}
    m["xT"] = np.ascontiguousarray(inputs["x"][b].T)
    m["l0_w_in"] = inputs["l0_w_in"]
    m["l0_mu"] = vec_l(inputs["l0_shift_mu"], 52)
    m["l0_w_out"] = inputs["l0_w_out"]
    m["l0_decay_up"] = inputs["l0_decay_up"]
    m["l0_iclr_up"] = inputs["l0_iclr_up"]
    m["l0_gate_up"] = inputs["l0_gate_up"]
    z16 = np.zeros(RW, np.float32)
    m["l0_vecs"] = np.ascontiguousarray(np.stack(
        [vec_l(inputs[n], 16) for n in ("l0_decay_base", "l0_iclr_base", "l0_k_k", "l0_k_a")]
        + [vec_l(inputs["l0_r_k"].reshape(-1), 16), vec_l(z16, 16), vec_l(z16, 16)], axis=1))
    m["l0_lnx"] = np.ascontiguousarray(np.stack([inputs["l0_lnx_g"].reshape(32, 64).T, inputs["l0_lnx_b"].reshape(32, 64).T], axis=1))
    for L in ("l0", "l1"):
        for n in ("ln_mix_g", "ln_mix_b", "ln_ffn_g", "ln_ffn_b"):
            m[f"{L}_{n}"] = vec_l(inputs[f"{L}_{n}"], 32)
        m[f"{L}_ffn_up"] = inputs[f"{L}_ffn_up"]
        m[f"{L}_ffn_down"] = inputs[f"{L}_ffn_down"]
        cw = inputs[f"{L}_ffn_conv_w"]
        m[f"{L}_ffn_cw"] = np.ascontiguousarray(np.stack([vec_l(cw[k], 172) for k in range(3)], axis=1))
        m[f"{L}_ffn_cb"] = vec_l(inputs[f"{L}_ffn_conv_b"], 172)
    m["l1_w_in"] = inputs["l1_w_in"]
    m["l1_w_out"] = inputs["l1_w_out"]
    m["l1_cw"] = np.ascontiguousarray(np.stack([vec_l(inputs["l1_conv_w"][k], 32) for k in range(4)], axis=1))
    m["l1_cb"] = vec_l(inputs["l1_conv_b"], 32)
    m["l1_gate_r_w"] = inputs["l1_gate_r_w"]
    m["l1_gate_i_w"] = inputs["l1_gate_i_w"]
    m["l1_rb"] = vec_l(inputs["l1_gate_r_b"], 32)
    m["l1_ib"] = vec_l(inputs["l1_gate_i_b"], 32)
    m["l1_lam"] = vec_l(inputs["l1_lambda"], 32)
    return m


def kernel(**inputs):
    n = 8
    nc = build()
    shared = prep_inputs(inputs, 0)
    in_maps = []
    for b in range(n):
        m = dict(shared)
        m["xT"] = np.ascontiguousarray(inputs["x"][b].T)
        in_maps.append(m)
    res = run_bass_kernel_spmd(nc, in_maps, core_ids=list(range(n)))
    out = np.stack([np.ascontiguousarray(res.results[b]["outT"].T) for b in range(n)], axis=0)
    return out.astype(np.float32, copy=False)
```
